# Optimizing a Trainium2 kernel written in Bass

```python
import jax, jax.numpy as jnp
from jax import lax
import numpy as np

D_MODEL = 1024
BATCH = 8
SEQ = 4096
DEPTH = 1

HEAD_DIM = 64
ATT_WIDTH = D_MODEL // 2
N_ATT_HEADS = ATT_WIDTH // HEAD_DIM
DIL_PATTERNS = ((128, 1), (512, 4), (2048, 16))
POOL_WIDTH = D_MODEL - ATT_WIDTH
POOL_WINDOWS = (2, 4, 8, 16)
N_POOL_GROUPS = len(POOL_WINDOWS)
POOL_GROUP_DIM = POOL_WIDTH // N_POOL_GROUPS
IN_WIDTH = 3 * ATT_WIDTH + POOL_WIDTH
MIX_WIDTH = ATT_WIDTH + POOL_WIDTH
D_FF = ((8 * D_MODEL // 3 + 127) // 128) * 128
CONV_WIDTH = 3
PLE_DIM = 256
EPS = 1e-6

kernel_name = "hybrid_dilated_attn_multiscale_pool_block"


def _rmsnorm(t, g):
    tf = t.astype(jnp.float32)
    inv = lax.rsqrt(jnp.mean(tf * tf, axis=-1, keepdims=True) + EPS)
    return (tf * inv * g.astype(jnp.float32)).astype(t.dtype)


def _alibi_slopes(n_heads):
    return jnp.exp2(-8.0 * (jnp.arange(n_heads, dtype=jnp.float32) + 1.0) / n_heads)


def _dilated_branch(q, k, v, slopes, window, dil):
    B, S, H, Dh = q.shape
    span = window // dil
    unit = span * dil
    Sp = -(-S // unit) * unit
    L = Sp // dil
    nb = L // span
    pad = ((0, 0), (0, Sp - S), (0, 0), (0, 0))

    def blocks(t):
        return jnp.pad(t, pad).reshape(B, nb, span, dil, H, Dh)

    def with_prev(t):
        prev = jnp.pad(t, ((0, 0), (1, 0), (0, 0), (0, 0), (0, 0), (0, 0)))[:, :nb]
        return jnp.concatenate([prev, t], axis=2)

    qb = blocks(q)
    kk = with_prev(blocks(k))
    vv = with_prev(blocks(v))
    s = jnp.einsum('bnqrhd,bnkrhd->bnrhqk', qb, kk,
                   preferred_element_type=jnp.float32)
    qi = jnp.arange(span)[:, None]
    kj = jnp.arange(2 * span)[None, :]
    diff = qi + span - kj
    blk = jnp.arange(nb)[:, None, None]
    valid = (diff >= 0) & (diff <= span) & (blk * span - span + kj[None] >= 0)
    dist = (diff * dil).astype(jnp.float32)
    s = s - slopes[:, None, None] * dist
    s = jnp.where(valid[:, None, None], s, -jnp.inf)
    m = jnp.max(s, axis=-1, keepdims=True)
    e = jnp.exp(s - m)
    den = jnp.sum(e, axis=-1)
    o = jnp.einsum('bnrhqk,bnkrhd->bnqrhd', e, vv.astype(jnp.float32))
    den_t = jnp.transpose(den, (0, 1, 4, 2, 3))
    m_t = jnp.transpose(m[..., 0], (0, 1, 4, 2, 3))
    o = o / den_t[..., None]
    o = o.reshape(B, Sp, H, Dh)[:, :S]
    return o, m_t.reshape(B, Sp, H)[:, :S], den_t.reshape(B, Sp, H)[:, :S]


def _dilated_attention(q, k, v, slopes):
    outs, maxes, dens = [], [], []
    for window, dil in DIL_PATTERNS:
        o, m, d = _dilated_branch(q, k, v, slopes, window, dil)
        outs.append(o)
        maxes.append(m)
        dens.append(d)
    o = jnp.stack(outs)
    m = jnp.stack(maxes)
    d = jnp.stack(dens)
    w = d * jnp.exp(m - jnp.max(m, axis=0, keepdims=True))
    w = w / jnp.sum(w, axis=0, keepdims=True)
    return jnp.sum(w[..., None] * o, axis=0)


def _multiscale_pool(u, pool_w, pool_scale):
    B, S, _ = u.shape
    ug = u.astype(jnp.float32).reshape(B, S, N_POOL_GROUPS, POOL_GROUP_DIM)
    cs0 = jnp.pad(jnp.cumsum(ug, axis=1), ((0, 0), (1, 0), (0, 0), (0, 0)))
    upper = cs0[:, 1:]
    t = jnp.arange(S)
    diffs = []
    for g, w in enumerate(POOL_WINDOWS):
        lower = jnp.pad(cs0[:, :, g], ((0, 0), (w - 1, 0), (0, 0)))[:, :S]
        count = jnp.minimum(t + 1, w).astype(jnp.float32)[None, :, None]
        diffs.append((upper[:, :, g] - lower) / count - ug[:, :, g])
    dlt = jnp.stack(diffs, axis=2).astype(u.dtype)
    y = jnp.einsum('bsgc,gce->bsge', dlt, pool_w)
    y = y * pool_scale.reshape(N_POOL_GROUPS, POOL_GROUP_DIM)
    return y.reshape(B, S, POOL_WIDTH)


def _shift(t, s):
    if s == 0:
        return t
    return jnp.pad(t, ((0, 0), (s, 0), (0, 0)))[:, :t.shape[1]]


def _causal_dwconv(t, w, b):
    y = b
    for kk in range(CONV_WIDTH):
        y = y + w[kk] * _shift(t, CONV_WIDTH - 1 - kk)
    return y


def setup_inputs(seed: int = 0) -> dict:
    key = jax.random.key(seed)
    ks = jax.random.split(key, 20)
    f32 = jnp.float32

    def nrm(k, shape, scale):
        return jax.random.normal(k, shape, f32) * scale

    return {
        "x": nrm(ks[0], (BATCH, SEQ, D_MODEL), 1.0),
        "p": nrm(ks[1], (DEPTH, BATCH, SEQ, PLE_DIM), 1.0),
        "ln_mix": 1.0 + nrm(ks[2], (DEPTH, D_MODEL), 0.02),
        "w_in": nrm(ks[3], (DEPTH, D_MODEL, IN_WIDTH), D_MODEL ** -0.5),
        "pool_w": nrm(ks[4], (DEPTH, N_POOL_GROUPS, POOL_GROUP_DIM, POOL_GROUP_DIM), POOL_GROUP_DIM ** -0.5),
        "pool_scale": 1.0 + nrm(ks[5], (DEPTH, POOL_WIDTH), 0.1),
        "w_out": nrm(ks[6], (DEPTH, MIX_WIDTH, D_MODEL), MIX_WIDTH ** -0.5),
        "ln_ffn": 1.0 + nrm(ks[7], (DEPTH, D_MODEL), 0.02),
        "w_up": nrm(ks[8], (DEPTH, D_MODEL, 2 * D_FF), D_MODEL ** -0.5),
        "conv_w": nrm(ks[9], (DEPTH, CONV_WIDTH, 2 * D_FF), CONV_WIDTH ** -0.5),
        "conv_b": nrm(ks[10], (DEPTH, 2 * D_FF), 0.02),
        "w_down": nrm(ks[11], (DEPTH, D_FF, D_MODEL), D_FF ** -0.5),
        "ln_ple": 1.0 + nrm(ks[12], (DEPTH, D_MODEL), 0.02),
        "w_ple_gate": nrm(ks[13], (DEPTH, D_MODEL, D_MODEL), D_MODEL ** -0.5),
        "w_ple": nrm(ks[14], (DEPTH, PLE_DIM, D_MODEL), PLE_DIM ** -0.5),
        "ln_final": 1.0 + nrm(ks[15], (D_MODEL,), 0.02),
    }


def reference(x, p, ln_mix, w_in, pool_w, pool_scale, w_out, ln_ffn, w_up, conv_w, conv_b,
              w_down, ln_ple, w_ple_gate, w_ple, ln_final):
    B, S, _ = x.shape
    slopes = _alibi_slopes(N_ATT_HEADS)
    h = x
    for i in range(DEPTH):
        hn = _rmsnorm(h, ln_mix[i])
        z = hn @ w_in[i]
        q = z[..., :ATT_WIDTH].reshape(B, S, N_ATT_HEADS, HEAD_DIM) * (HEAD_DIM ** -0.5)
        k = z[..., ATT_WIDTH:2 * ATT_WIDTH].reshape(B, S, N_ATT_HEADS, HEAD_DIM)
        v = z[..., 2 * ATT_WIDTH:3 * ATT_WIDTH].reshape(B, S, N_ATT_HEADS, HEAD_DIM)
        u = z[..., 3 * ATT_WIDTH:]
        att = _dilated_attention(q, k, v, slopes).reshape(B, S, ATT_WIDTH).astype(h.dtype)
        pool = _multiscale_pool(u, pool_w[i], pool_scale[i]).astype(h.dtype)
        h = h + jnp.concatenate([att, pool], axis=-1) @ w_out[i]
        hn = _rmsnorm(h, ln_ffn[i])
        up = _causal_dwconv(hn @ w_up[i], conv_w[i], conv_b[i])
        gate, val = jnp.split(up, 2, axis=-1)
        h = h + (jax.nn.silu(gate) * val) @ w_down[i]
        g = jax.nn.sigmoid(_rmsnorm(h, ln_ple[i]) @ w_ple_gate[i])
        h = h + g * (p[i] @ w_ple[i])
    return _rmsnorm(h, ln_final)
```

```python
import numpy as np
import concourse.bass as bass
import concourse.mybir as mybir
from concourse.bass_utils import run_bass_kernel_spmd

F32 = mybir.dt.float32
BF16 = mybir.dt.bfloat16
AF = mybir.ActivationFunctionType
ALU = mybir.AluOpType

S = 4096
D = 1024
NBLK = 32
NSPAN = 8
SP = 512
DFF = 2816
NJ = 22
PLE = 256
EPS = 1e-6
NEG = -30000.0

V_LNMIX, V_LNFFN, V_LNPLE, V_LNFIN, V_PSC, V_CW, V_CB, V_CNT, V_EPS, NV = 0, 8, 16, 24, 32, 36, 168, 212, 228, 229


class Tracker:
    ENG = ("pe", "act", "dve", "pool", "sp")

    def __init__(self):
        self.ops = []
        self.last_w = {}
        self.readers = {}
        self.pending = {}
        self.dma_since = []
        self.last_on = {}

    def add(self, eng, fn, reads=(), writes=(), dma=False, semkey=None):
        idx = len(self.ops)
        raw, other = set(), set()
        for k in reads:
            w = self.last_w.get(k)
            if w is not None:
                raw.add(w)
        for k in writes:
            w = self.last_w.get(k)
            if w is not None:
                other.add(w)
            other.update(self.readers.get(k, ()))
        if eng in self.pending:
            raw.update(self.pending.pop(eng))
        other -= raw
        self.ops.append(dict(eng=eng, fn=fn, raw=raw, other=other, dma=dma, semkey=semkey, idx=idx))
        for k in reads:
            self.readers.setdefault(k, []).append(idx)
        for k in writes:
            self.last_w[k] = idx
            self.readers[k] = []
        self.last_on[eng] = idx
        if dma:
            self.dma_since.append(idx)
        return idx

    def barrier(self):
        snap = set(self.last_on.values()) | set(self.dma_since)
        self.dma_since = []
        for e in self.ENG:
            self.pending[e] = set(snap) | self.pending.get(e, set())

    def emit(self, nc, block, sems, new_sem):
        ops = self.ops
        for op in ops:
            deps = set()
            for d in op["raw"] | op["other"]:
                dop = ops[d]
                if dop["fn"] is None:
                    continue
                if (not dop["dma"]) and dop["eng"] == op["eng"]:
                    if op["eng"] == "pe" or d not in op["raw"]:
                        continue
                deps.add(d)
            op["deps"] = deps
        marked = set()
        for op in ops:
            marked.update(op["deps"])
        cnt = {e: 0 for e in self.ENG}
        dcnt = {}
        dsem = {}
        for op in ops:
            if op["dma"]:
                k = (op["semkey"], op["eng"])
                dcnt[k] = dcnt.get(k, 0) + 16
                op["val"] = dcnt[k]
                if k not in dsem:
                    dsem[k] = new_sem("d%d" % len(dsem))
                op["sem"] = dsem[k]
            elif op["idx"] in marked:
                cnt[op["eng"]] += 1
                op["val"] = cnt[op["eng"]]
                op["sem"] = sems[op["eng"]]
        per_eng = {e: [op for op in ops if op["eng"] == e] for e in self.ENG}
        know = {e: {} for e in self.ENG}
        semobj = {}
        for op in ops:
            e = op["eng"]
            kn = know[e]
            waits = []
            for d in sorted(op["deps"]):
                dop = ops[d]
                sid = id(dop["sem"])
                semobj[sid] = dop["sem"]
                if kn.get(sid, 0) >= dop["val"]:
                    continue
                waits.append((dop["sem"], dop["val"]))
                for k2, v2 in dop["vc"].items():
                    if kn.get(k2, 0) < v2:
                        kn[k2] = v2
            wmax = {}
            for s_, v_ in waits:
                if wmax.get(id(s_), (None, 0))[1] < v_:
                    wmax[id(s_)] = (s_, v_)
            op["waits"] = list(wmax.values())
            vc = dict(kn)
            if "val" in op:
                sid = id(op["sem"])
                if vc.get(sid, 0) < op["val"]:
                    vc[sid] = op["val"]
                if not op["dma"]:
                    kn[sid] = max(kn.get(sid, 0), 0)
            op["vc"] = vc

        def run(engname, eh):
            for op in per_eng[engname]:
                for (s, v) in op["waits"]:
                    eh.wait_ge(s, v)
                if op["fn"] is None:
                    continue
                ins = op["fn"](eh)
                if op["dma"]:
                    ins.then_inc(op["sem"], 16)
                elif op["idx"] in marked:
                    ins.then_inc(op["sem"], 1)

        @block.tensor
        def _(e):
            run("pe", e)

        @block.scalar
        def _(e):
            run("act", e)

        @block.vector
        def _(e):
            run("dve", e)

        @block.gpsimd
        def _(e):
            run("pool", e)

        @block.sync
        def _(e):
            run("sp", e)


def build_program():
    nc = bass.Bass("TRN2", target_bir_lowering=False)
    dt_in = lambda n, shp: nc.dram_tensor(n, shp, F32, kind="ExternalInput").ap()
    x = dt_in("x", [S, D])
    p = dt_in("p", [S, PLE])
    w_in = dt_in("w_in", [D, 2048])
    w_out = dt_in("w_out", [D, D])
    w_up = dt_in("w_up", [D, 2 * DFF])
    w_down = dt_in("w_down", [DFF, D])
    w_gate = dt_in("w_gate", [D, D])
    w_ple = dt_in("w_ple", [PLE, D])
    pool_w = dt_in("pool_w", [4, 128, 128])
    vecs = dt_in("vecs", [128, NV])
    ident = dt_in("ident", [128, 128])
    bias_tab = dt_in("bias_tab", [128, 12 * 256])
    out = nc.dram_tensor("out", [S, D], F32, kind="ExternalOutput").ap()
    s_in = nc.dram_tensor("s_in", [12, 128, 8 * 128], BF16, kind="Internal").ap()
    s_out = nc.dram_tensor("s_out", [8, 128, 8 * 128], BF16, kind="Internal").ap()
    s_up = nc.dram_tensor("s_up", [NJ, 128, 8 * 2 * 128], BF16, kind="Internal").ap()
    s_down = nc.dram_tensor("s_down", [8, 128, NJ * 128], BF16, kind="Internal").ap()
    s_gate = nc.dram_tensor("s_gate", [8, 128, 8 * 128], BF16, kind="Internal").ap()
    s_ple = nc.dram_tensor("s_ple", [8, 128, 2 * 128], BF16, kind="Internal").ap()

    T = Tracker()
    from contextlib import ExitStack
    es = ExitStack()
    with es:
        sb = lambda n, shp, d: es.enter_context(nc.sbuf_tensor(n, shp, d))
        ps = lambda n, shp, d: es.enter_context(nc.psum_tensor(n, shp, d))
        QT = sb("QT", [128, 4 * S], BF16)
        KT = sb("KT", [128, 4 * S], BF16)
        VLEN = 64 + NBLK * 512 + 64
        V1 = sb("V1", [128, VLEN], BF16)
        PT_ = sb("poolT", [128, 4 * S], BF16)
        xst = sb("xst", [128, 4 * D], F32)
        identf = sb("identf", [128, 128], F32)[:, :]
        identb = sb("identb", [128, 128], BF16)[:, :]
        onesb = sb("onesb", [128, 128], BF16)[:, :]
        vec = sb("vec", [128, NV], F32)[:, :]
        AR_N = 13312
        arf_t = sb("arena", [128, AR_N], F32)
        arf = arf_t[:, :]
        arb = arf.bitcast(BF16)
        QTa, KTa, V1a, xsta, PTa = QT[:, :], KT[:, :], V1[:, :], xst[:, :], PT_[:, :]
        ringA = [arb[:, i * 1024:(i + 1) * 1024] for i in range(4)]
        wv = arb[:, 4096:8192]
        xnb = [arb[:, 8192 + i * 1024: 8192 + (i + 1) * 1024] for i in range(2)]
        junk = arb[:, 10240:11264]
        xnT = arb[:, 11264:15360]
        Gb = arb[:, 15360:16384]
        dlt = arb[:, 16384:18432]
        poolw = arb[:, 18432:18944]
        Ut = [arf[:, 9472 + g * 528: 9472 + (g + 1) * 528] for g in range(4)]
        St = [arf[:, 11584 + i * 528: 11584 + (i + 1) * 528] for i in range(3)]
        ss = arf[:, 13168:13200]
        sq_ = arf[:, 13200:13232]
        rstd = arf[:, 13232:13264]
        V4LEN = 64 + 32 * 128 + 64
        V16LEN = 64 + 32 * 256 + 64
        V4c = arb[:, 0:V4LEN]
        V16 = arb[:, 4224:4224 + V16LEN]
        PTg = [arb[:, 12544 + g * 1024: 12544 + (g + 1) * 1024] for g in range(2)]
        PTr = [PTg[i // 4][:, (i % 4) * 256:(i % 4 + 1) * 256] for i in range(8)]
        biasb = arb[:, 14592:14592 + 3072]
        acc = [arf[:, 8832 + i * 2048: 8832 + (i + 1) * 2048] for i in range(2)]
        rden = xsta[:, 0:2048]
        sqb = arb[:, 0:4096]
        hT = arf[:, 2048:6144]
        rsb = arf[:, 6144:6656]
        cv = [arf[:, 6656 + i * 514: 6656 + (i + 1) * 514] for i in range(8)]
        ost = [arf[:, 10768 + i * 1024: 10768 + (i + 1) * 1024] for i in range(2)]
        CG = arf[:, 12816:12816 + 176]
        sgt = [cv[4], cv[5]]
        actT = KTa[:, 0:NJ * 512]
        hnT = KTa[:, 11264:15360]
        pT = KTa[:, 15360:16384]
        RSL = 2816
        ring = [V1a[:, i * RSL:(i + 1) * RSL] for i in range(5)]
        pst = V1a[:, 14080:16128].bitcast(F32)
        pbank = [ps("pb%d" % i, [128, 512], F32) for i in range(6)]
        pbig = ps("pbig", [128, 1024], F32)
        pbig_b = pbig[:, :].bitcast(BF16)

        sems = {e: es.enter_context(nc.semaphore("s_" + e)) for e in Tracker.ENG}
        new_sem = lambda n: es.enter_context(nc.semaphore(n))

        pbc = [0]

        def next_bank(avoid=()):
            while True:
                i = pbc[0] % 6
                pbc[0] += 1
                if i not in avoid:
                    return i

        def v3(ap, a):
            return ap.rearrange("p (a b) -> p a b", a=a)

        T.add("sp", lambda e: e.dma_start(out=identf[:, :], in_=ident[:, :]), writes=["identf"], dma=True, semkey="setup")
        T.add("sp", lambda e: e.dma_start(out=vec[:, :], in_=vecs[:, :]), writes=["vec"], dma=True, semkey="setup")
        T.add("pool", lambda e: e.dma_start(out=identb[:, :], in_=ident[:, :]), writes=["identb"], dma=True, semkey="setup")
        T.add("pool", lambda e: e.dma_start(out=poolw.rearrange("c (g e) -> c g e", g=4),
                                            in_=pool_w.rearrange("g c e -> c g e")), writes=["poolw"], dma=True, semkey="setup")
        T.add("pool", lambda e: e.dma_start(out=wv.rearrange("p (k n) -> p k n", k=8),
                                            in_=w_in[:, 1024:1536].rearrange("(k p) n -> p k n", p=128)), writes=["wv"], dma=True, semkey="setup")
        for j in range(12):
            col0 = (j if j < 8 else j + 4) * 128
            T.add("pool", lambda e, j=j, col0=col0: e.dma_start(out=s_in[j].rearrange("p (k n) -> p k n", k=8),
                                                                in_=w_in[:, col0:col0 + 128].rearrange("(k p) n -> p k n", p=128)),
                  writes=[("s_in", j)], dma=True, semkey="setup")
        T.add("dve", lambda e: e.memset(onesb[:, :], 1.0), writes=["onesb"])
        T.add("dve", lambda e: e.memset(V1a[:, 0:64], 1.0), writes=["V1ones"])
        T.add("dve", lambda e: e.memset(V1a[:, VLEN - 64:VLEN], 1.0), writes=["V1ones2"])
        T.add("dve", lambda e: e.memset(Gb, 1.0), writes=["Gb"])
        for g in range(4):
            T.add("pool", lambda e, g=g: e.memset(Ut[g][:, 0:16], 0.0), writes=[("U", g)])
        for i in range(3):
            T.add("pool", lambda e, i=i: e.memset(St[i], 0.0), writes=[("St", i)])
        T.barrier()
        for c in range(8):
            T.add("dve", lambda e, c=c: e.tensor_scalar(out=Gb[:, c * 128:(c + 1) * 128], in0=Gb[:, c * 128:(c + 1) * 128],
                                                        scalar1=vec[:, V_LNMIX + c:V_LNMIX + c + 1], scalar2=None, op0=ALU.mult),
                  reads=["vec"], writes=["Gb"])
        prep = []
        for j in range(8):
            prep.append((lambda e, j=j: e.dma_start(out=s_out[j].rearrange("p (k n) -> p k n", k=8),
                                                    in_=w_out[:, j * 128:(j + 1) * 128].rearrange("(k p) n -> p k n", p=128)), ("s_out", j)))
        for j in range(NJ):
            for h in range(2):
                prep.append((lambda e, j=j, h=h: e.dma_start(
                    out=s_up[j].rearrange("p (k h n) -> p k h n", k=8, h=2)[:, :, h, :],
                    in_=w_up[:, h * DFF + j * 128: h * DFF + (j + 1) * 128].rearrange("(k p) n -> p k n", p=128)), ("s_up", j, h)))
        for j in range(8):
            prep.append((lambda e, j=j: e.dma_start(out=s_down[j].rearrange("p (k n) -> p k n", k=NJ),
                                                    in_=w_down[:, j * 128:(j + 1) * 128].rearrange("(k p) n -> p k n", p=128)), ("s_down", j)))
        for j in range(8):
            prep.append((lambda e, j=j: e.dma_start(out=s_gate[j].rearrange("p (k n) -> p k n", k=8),
                                                    in_=w_gate[:, j * 128:(j + 1) * 128].rearrange("(k p) n -> p k n", p=128)), ("s_gate", j)))
            prep.append((lambda e, j=j: e.dma_start(out=s_ple[j].rearrange("p (k n) -> p k n", k=2),
                                                    in_=w_ple[:, j * 128:(j + 1) * 128].rearrange("(k p) n -> p k n", p=128)), ("s_ple", j)))

        def issue_prep(nmax):
            for _ in range(nmax):
                if prep:
                    fn, key = prep.pop(0)
                    T.add("pool", fn, writes=[key], dma=True, semkey="prep")

        xnTs = [xnT, xsta[:, 2048:4096].bitcast(BF16)]

        def load_x(gb):
            sl = gb % 2
            T.add("pool", lambda e: e.dma_start(out=xst[:, sl * D:(sl + 1) * D], in_=x[gb * 128:(gb + 1) * 128, :]),
                  writes=[("xs", sl)], dma=True, semkey=("xs", sl))

        for gb in range(2):
            load_x(gb)
        rca = [0]

        def ringA_load(j):
            sl = rca[0] % 4
            rca[0] += 1
            T.add("sp", lambda e: e.dma_start(out=ringA[sl], in_=s_in[j]), reads=[("s_in", j)], writes=[("ringA", sl)],
                  dma=True, semkey=("ringA", sl))
            return sl

        def norm_block(s, b):
            gb = 4 * s + b
            sl = gb % 2
            xs = xst[:, sl * D:(sl + 1) * D]
            xb = xnb[gb % 2]
            xt = xnTs[s % 2]
            T.add("act", lambda e: e.activation(out=junk, in_=xs, func=AF.Square, accum_out=ss[:, gb:gb + 1]),
                  reads=[("xs", sl)], writes=["junk", ("ss", gb)])
            T.add("act", lambda e: e.activation(out=sq_[:, gb:gb + 1], in_=ss[:, gb:gb + 1], func=AF.Sqrt,
                                                bias=vec[:, V_EPS:V_EPS + 1], scale=1.0 / D),
                  reads=[("ss", gb), "vec"], writes=[("sq", gb)])
            T.add("dve", lambda e: e.reciprocal(out=rstd[:, gb:gb + 1], in_=sq_[:, gb:gb + 1]),
                  reads=[("sq", gb)], writes=[("rstd", gb)])
            T.add("dve", lambda e: e.tensor_scalar(out=xb, in0=xs, scalar1=rstd[:, gb:gb + 1], scalar2=None, op0=ALU.mult),
                  reads=[("xs", sl), ("rstd", gb)], writes=[("xnb", gb % 2)])

            def tr(e):
                ins = None
                for c in range(8):
                    ins = e.transpose(out=pbig_b[:, c * 128:(c + 1) * 128], in_=xb[:, c * 128:(c + 1) * 128], identity=identb[:, :])
                return ins
            T.add("pe", tr, reads=[("xnb", gb % 2), "identb"], writes=["pbig"])
            T.add("dve", lambda e: e.tensor_tensor(out=v3(xt, 8)[:, :, b * 128:(b + 1) * 128], in0=v3(pbig_b[:, 0:1024], 8),
                                                   in1=v3(Gb, 8), op=ALU.mult),
                  reads=["pbig", "Gb"], writes=[("xnT", s % 2, b)])
            if gb + 2 < NBLK:
                load_x(gb + 2)

        def proj_task(s, t):
            xt = xnTs[s % 2]
            xk = [("xnT", s % 2, b) for b in range(4)]
            if t < 12:
                j = t
                rsl = ringA_load(j)
                bk = next_bank()

                def mm(e):
                    ins = None
                    for kc in range(8):
                        ins = e.matmul(pbank[bk][:, :], lhsT=ringA[rsl][:, kc * 128:(kc + 1) * 128],
                                       rhs=xt[:, kc * 512:(kc + 1) * 512], start=(kc == 0), stop=(kc == 7))
                    return ins
                T.add("pe", mm, reads=xk + [("ringA", rsl)], writes=[("pb", bk)])
                if j < 4:
                    T.add("act", lambda e: e.activation(out=QT[:, j * S + s * SP: j * S + (s + 1) * SP], in_=pbank[bk][:, :],
                                                        func=AF.Copy, scale=0.125),
                          reads=[("pb", bk)], writes=[("QT", j, s, 0), ("QT", j, s, 1)])
                elif j < 8:
                    c = j - 4
                    T.add("dve", lambda e: e.tensor_copy(out=KT[:, c * S + s * SP: c * S + (s + 1) * SP], in_=pbank[bk][:, :]),
                          reads=[("pb", bk)], writes=[("KT", c, s)])
                else:
                    g = j - 8
                    T.add("act", lambda e: e.activation(out=Ut[g][:, 16:528], in_=pbank[bk][:, :], func=AF.Copy),
                          reads=[("pb", bk)], writes=[("U", g)])
                    src = Ut[g]
                    for lvl in range(g + 1):
                        sh = 1 << lvl
                        dst = St[lvl % 2]
                        T.add("dve", lambda e, src=src, dst=dst, sh=sh: e.tensor_tensor(out=dst[:, sh:528], in0=src[:, sh:528], in1=src[:, 0:528 - sh], op=ALU.add),
                              reads=[("U", g), ("St", (lvl - 1) % 2)] if lvl else [("U", g)], writes=[("St", lvl % 2)])
                        src = dst
                    w = 2 << g
                    kst = ("St", g % 2)
                    T.add("dve", lambda e: e.scalar_tensor_tensor(out=dlt[:, g * 512:(g + 1) * 512], in0=src[:, 16:528], scalar=1.0 / w,
                                                                  in1=Ut[g][:, 16:528], op0=ALU.mult, op1=ALU.subtract),
                          reads=[("U", g), kst], writes=[("dlt", g)])
                    if s == 0:
                        T.add("dve", lambda e: e.tensor_tensor(out=src[:, 16:16 + w - 1], in0=src[:, 16:16 + w - 1],
                                                               in1=vec[:, V_CNT:V_CNT + w - 1], op=ALU.mult),
                              reads=[kst, ("dlt", g), "vec"], writes=[kst])
                        T.add("dve", lambda e: e.tensor_tensor(out=dlt[:, g * 512: g * 512 + w - 1], in0=src[:, 16:16 + w - 1],
                                                               in1=Ut[g][:, 16:16 + w - 1], op=ALU.subtract),
                              reads=[kst, ("U", g)], writes=[("dlt", g)])
                    T.add("dve", lambda e: e.tensor_copy(out=Ut[g][:, 0:16], in_=Ut[g][:, 512:528]),
                          reads=[("U", g), ("dlt", g), ("St", 0), ("St", 1)], writes=[("U", g)])
                    bk2 = next_bank()
                    T.add("pe", lambda e: e.matmul(pbank[bk2][:, :], lhsT=poolw[:, g * 128:(g + 1) * 128],
                                                   rhs=dlt[:, g * 512:(g + 1) * 512], start=True, stop=True),
                          reads=[("dlt", g), "poolw"], writes=[("pb", bk2)])
                    T.add("act", lambda e: e.activation(out=PT_[:, g * S + s * SP: g * S + (s + 1) * SP], in_=pbank[bk2][:, :],
                                                        func=AF.Copy, scale=vec[:, V_PSC + g:V_PSC + g + 1]),
                          reads=[("pb", bk2), "vec"], writes=[("poolT", g, s)])
            else:
                b = t - 12
                gb = 4 * s + b
                bk = next_bank()

                def mmv(e):
                    ins = None
                    for kc in range(8):
                        ins = e.matmul(pbank[bk][:, :], lhsT=xt[:, kc * 512 + b * 128: kc * 512 + (b + 1) * 128],
                                       rhs=wv[:, kc * 512:(kc + 1) * 512], start=(kc == 0), stop=(kc == 7))
                    return ins
                T.add("pe", mmv, reads=xk + ["wv"], writes=[("pb", bk)])
                if b % 2 == 0:
                    T.add("act", lambda e: e.activation(out=V1[:, 64 + gb * 512: 64 + (gb + 1) * 512], in_=pbank[bk][:, :], func=AF.Copy),
                          reads=[("pb", bk)], writes=[("V1", gb)])
                else:
                    T.add("dve", lambda e: e.tensor_copy(out=V1[:, 64 + gb * 512: 64 + (gb + 1) * 512], in_=pbank[bk][:, :]),
                          reads=[("pb", bk)], writes=[("V1", gb)])

        for b in range(4):
            norm_block(0, b)
        for s in range(NSPAN):
            for t in range(16):
                proj_task(s, t)
                issue_prep(1)
                if t % 4 == 3 and s + 1 < NSPAN:
                    norm_block(s + 1, t // 4)
        issue_prep(1000)
        T.barrier()

        T.add("pool", lambda e: e.dma_start(out=biasb, in_=bias_tab[:, :]), writes=["biasb"], dma=True, semkey="setupB")
        T.add("dve", lambda e: e.memset(V4c[:, 0:64], 1.0), writes=["V4ones"])
        T.add("dve", lambda e: e.memset(V4c[:, V4LEN - 64:V4LEN], 1.0), writes=["V4ones"])
        T.add("dve", lambda e: e.memset(V16[:, 0:64], 1.0), writes=["V16ones"])
        T.add("dve", lambda e: e.memset(V16[:, V16LEN - 64:V16LEN], 1.0), writes=["V16ones"])
        v1keys = [("V1", gb) for gb in range(NBLK)]

        def mk_ap(t, part0, nparts, off, dims):
            rowlen = t.ap[0][0]
            return bass.AP(t.tensor, t.offset + part0[0] * rowlen + off, [[rowlen * part0[1], nparts]] + [list(d) for d in dims])

        def load_v4(c):
            for r in range(4):
                for a in range(4):
                    dst = mk_ap(V4c, (32 * a, 1), 32, 64 + r * 128, [[512, 8], [1, 128]])
                    src = mk_ap(V1a, (r, 4), 32, 64 + a * 512 + c * 128, [[2048, 8], [1, 128]])
                    T.add("sp", lambda e, dst=dst, src=src: e.dma_start(out=dst, in_=src), reads=v1keys, writes=[("V4c", r, a)],
                          dma=True, semkey="V4c")
        def load_v16(hf):
            for r in range(16):
                for a in range(16):
                    dst = mk_ap(V16, (8 * a, 1), 8, 64 + r * 256, [[16 * 256, 2], [1, 256]])
                    src = mk_ap(V1a, (r, 16), 8, 64 + a * 512 + hf * 256, [[16 * 512, 2], [1, 256]])
                    T.add("sp" if (a % 2 == 0) else "pool", lambda e, dst=dst, src=src: e.dma_start(out=dst, in_=src), reads=v1keys,
                          writes=[("V16", r, a)], dma=True, semkey=("V16", r, a % 2))

        def vones(t, tlen, voff, hh):
            rowlen = t.ap[0][0]
            if hh == 0:
                first, second = voff, tlen - 64
            else:
                first, second = 0, voff
            return bass.AP(t.tensor, t.offset + first, [[rowlen, 128], [second - first, 2], [1, 64]])

        stc = [0]
        ptc = [0]
        ob = [0]
        hu = [0]
        for c in range(4):
            load_v4(c)
            if c % 2 == 0:
                load_v16(c // 2)
            v4keys = [("V4c", r, a) for r in range(4) for a in range(4)]
            for n in range(2):
                for hh in range(2):
                    h = 2 * c + hh
                    asl = hu[0] % 2
                    hu[0] += 1
                    p0 = 64 * hh
                    tasks = []
                    for d in (1, 4, 16):
                        e_idx = (-(h + 1) + {1: 0, 4: 2, 16: 4}[d]) + 8
                        nper = 16 // d
                        groups = []
                        if d == 1:
                            for g4 in range(4):
                                groups.append([(0, n * 16 + g4 * 4 + i) for i in range(4)])
                        elif d == 4:
                            for jj in range(4):
                                groups.append([(r, n * 4 + jj) for r in range(4)])
                        else:
                            for g4 in range(4):
                                groups.append([(g4 * 4 + i, n) for i in range(4)])
                        for gi, grp in enumerate(groups):
                            for qi, (r, j) in enumerate(grp):
                                tasks.append(dict(d=d, r=r, j=j, e=e_idx, gi=gi, qi=qi, last=(qi == 3),
                                                  nprev=sum(1 for (_, jj_) in grp if jj_ > 0)))
                    LAG = 3
                    state = {}

                    def emit_st(t):
                        d, r, j = t["d"], t["r"], t["j"]
                        st = stc[0] % 4
                        stc[0] += 1
                        if t["qi"] == 0:
                            state["ptg"] = ptc[0] % 2
                            ptc[0] += 1
                        pt = state["ptg"] * 4 + t["qi"]
                        t["pt"] = pt
                        t["ptg"] = state["ptg"]
                        bank, half = 2 + st, 0
                        ncol = 256 if j > 0 else 128
                        t["ncol"] = ncol
                        qbase = c * S + 128 * d * j + r
                        qap = mk_ap(QTa, (p0, 1), 64, qbase, [[d, 128]])
                        kcur = mk_ap(KTa, (p0, 1), 64, qbase, [[d, 128]])
                        kprev = mk_ap(KTa, (p0, 1), 64, qbase - 128 * d, [[d, 128]]) if j > 0 else None
                        e_idx = t["e"]

                        def f(e):
                            o = pbank[bank]
                            ins = e.matmul(o[:, half:half + 128], lhsT=kcur, rhs=qap, start=True, stop=(kprev is None), skip_group_check=True)
                            if kprev is not None:
                                ins = e.matmul(o[:, half + 128:half + 256], lhsT=kprev, rhs=qap, start=False, stop=True, skip_group_check=True)
                            return ins
                        s0 = (128 * d * j) // SP
                        qk = [("QT", c, sx, hh) for sx in range(8)] if d == 16 else [("QT", c, s0, hh)]
                        kk = [("KT", c, sx) for sx in range(8)] if d == 16 else [("KT", c, s0), ("KT", c, max(s0 - 1, 0))]
                        T.add("pe", f, reads=qk + kk, writes=[("ST", st)])
                        T.add("act", lambda e: e.activation(out=PTr[pt][:, 0:ncol], in_=pbank[bank][:, half:half + ncol], func=AF.Exp),
                              reads=[("ST", st)], writes=[("PT", pt)])
                        T.add("dve", lambda e: e.tensor_tensor(out=PTr[pt][:, 0:ncol], in0=PTr[pt][:, 0:ncol],
                                                               in1=biasb[:, e_idx * 256: e_idx * 256 + ncol], op=ALU.mult),
                              reads=[("PT", pt), "biasb"], writes=[("PT", pt)])

                    def emit_pv(t):
                        d, r, j = t["d"], t["r"], t["j"]
                        if t["qi"] == 0:
                            state["ob"] = ob[0] % 2
                            ob[0] += 1
                        obk = state["ob"]
                        pt = t["pt"]
                        qi = t["qi"]
                        if d == 1:
                            tv, tl = V1a, VLEN
                            off = lambda jj: 64 + jj * 512 + h * 64
                            vk = lambda jj: [("V1", jj)]
                            ok = ["V1ones", "V1ones2"]
                        elif d == 4:
                            tv, tl = V4c, V4LEN
                            off = lambda jj: 64 + (jj * 4 + r) * 128 + hh * 64
                            vk = lambda jj: v4keys
                            ok = ["V4ones"]
                        else:
                            tv, tl = V16, V16LEN
                            off = lambda jj: 64 + (jj * 16 + r) * 256 + (h % 4) * 64
                            vk = lambda jj: [("V16", r, a) for a in range(16)]
                            ok = ["V16ones"]
                        lc = tv[:, off(j):off(j) + 64]
                        lp = tv[:, off(j - 1):off(j - 1) + 64] if j > 0 else None
                        np0, dp0 = (0, 64) if hh == 0 else (64, 0)

                        nprev = t["nprev"]
                        batched = nprev in (0, 4)
                        ptg = t["ptg"]

                        def f(e):
                            on = pbank[obk][np0:np0 + 64, qi * 128:(qi + 1) * 128]
                            od = pbank[obk][dp0:dp0 + 64, qi * 128:(qi + 1) * 128]
                            ins = e.matmul(on, lhsT=lc, rhs=PTr[pt][:, 0:128], start=(qi == 0), stop=(lp is None), skip_group_check=True)
                            if lp is not None:
                                ins = e.matmul(on, lhsT=lp, rhs=PTr[pt][:, 128:256], start=False, stop=True, skip_group_check=True)
                            if not batched:
                                ins = e.matmul(od, lhsT=onesb[:, 0:64], rhs=PTr[pt][:, 0:128], start=(qi == 0), stop=(lp is None), skip_group_check=True)
                                if lp is not None:
                                    ins = e.matmul(od, lhsT=onesb[:, 0:64], rhs=PTr[pt][:, 128:256], start=False, stop=True, skip_group_check=True)
                            elif qi == 3:
                                odb = pbank[obk][dp0:dp0 + 64, :]
                                pg = v3(PTg[ptg], 4)
                                ins = e.matmul(odb, lhsT=onesb[:, 0:64], rhs=pg[:, :, 0:128], start=True, stop=(nprev == 0), skip_group_check=True)
                                if nprev == 4:
                                    ins = e.matmul(odb, lhsT=onesb[:, 0:64], rhs=pg[:, :, 128:256], start=False, stop=True, skip_group_check=True)
                            return ins
                        ptk = [("PT", ptg * 4 + i) for i in range(4)] if (batched and qi == 3) else [("PT", pt)]
                        T.add("pe", f, reads=ptk + vk(j) + (vk(j - 1) if j > 0 and d == 1 else []) + ok + ["onesb"], writes=[("OB", obk)])
                        if t["last"]:
                            gi = t["gi"]
                            if d == 1:
                                a_ap = acc[asl][:, gi * 512:(gi + 1) * 512]
                                T.add("act", lambda e: e.activation(out=a_ap, in_=pbank[obk][:, :], func=AF.Copy),
                                      reads=[("OB", obk)], writes=[("acc", asl)])
                            else:
                                if d == 4:
                                    a_ap = mk_ap(acc[asl], (0, 1), 128, gi * 512, [[1, 4], [4, 128]])
                                else:
                                    a_ap = mk_ap(acc[asl], (0, 1), 128, gi * 4, [[1, 4], [16, 128]])
                                T.add("dve", lambda e: e.tensor_tensor(out=a_ap, in0=v3(pbank[obk][:, :], 4), in1=a_ap, op=ALU.add),
                                      reads=[("OB", obk), ("acc", asl)], writes=[("acc", asl)])

                    for i in range(len(tasks) + LAG):
                        if i < len(tasks):
                            emit_st(tasks[i])
                        if i >= LAG:
                            emit_pv(tasks[i - LAG])
                    np0, dp0 = (0, 64) if hh == 0 else (64, 0)
                    T.add("act", lambda e, asl=asl, np0=np0, dp0=dp0: e.activation(out=rden[np0:np0 + 64, :], in_=acc[asl][dp0:dp0 + 64, :], func=AF.Ln),
                          reads=[("acc", asl)], writes=[("rden", hh)])
                    T.add("act", lambda e, np0=np0: e.activation(out=rden[np0:np0 + 64, :], in_=rden[np0:np0 + 64, :], func=AF.Exp, scale=-1.0),
                          reads=[("rden", hh)], writes=[("rden", hh)])
                    wk = [("QT", c, sx, hh) for sx in range(4 * n, 4 * n + 4)]
                    T.add("pool", lambda e, asl=asl, np0=np0, c=c, n=n: e.tensor_tensor(
                        out=QT[np0:np0 + 64, c * S + n * 2048: c * S + (n + 1) * 2048], in0=acc[asl][np0:np0 + 64, :],
                        in1=rden[np0:np0 + 64, :], op=ALU.mult),
                        reads=[("acc", asl), ("rden", hh)], writes=[("att", c, hh, n)] + wk)
        T.barrier()

        T.add("dve", lambda e: e.memset(CG, 0.0), writes=["CG"])
        rc = [0]

        def ring_load(src_ap, nel, key_reads):
            sl = rc[0] % 5
            rc[0] += 1
            T.add("sp", lambda e: e.dma_start(out=ring[sl][:, 0:nel], in_=src_ap), reads=key_reads, writes=[("ring", sl)],
                  dma=True, semkey=("ring", sl))
            return sl

        def load_xC(s):
            for b in range(4):
                gb = 4 * s + b
                T.add("sp", lambda e, b=b, gb=gb: e.dma_start(out=xst[:, b * D:(b + 1) * D], in_=x[gb * 128:(gb + 1) * 128, :]),
                      writes=[("xs", b)], dma=True, semkey=("xs", b))

        def load_p(s):
            for b in range(4):
                gb = 4 * s + b
                T.add("pool", lambda e, b=b, gb=gb: e.dma_start(out=pst[:, b * 256:(b + 1) * 256], in_=p[gb * 128:(gb + 1) * 128, :]),
                      writes=[("pst", b)], dma=True, semkey=("pst", b))

        def rmsnorm(gcol, out_bf, avoid=()):
            for c in range(8):
                if c % 2 == 0:
                    T.add("act", lambda e, c=c: e.activation(out=sqb[:, c * 512:(c + 1) * 512], in_=hT[:, c * 512:(c + 1) * 512], func=AF.Square),
                          reads=[("hT", c)], writes=[("sqb", c)])
                else:
                    T.add("dve", lambda e, c=c: e.tensor_tensor(out=sqb[:, c * 512:(c + 1) * 512], in0=hT[:, c * 512:(c + 1) * 512],
                                                                in1=hT[:, c * 512:(c + 1) * 512], op=ALU.mult),
                          reads=[("hT", c)], writes=[("sqb", c)])
            bk = next_bank(avoid)

            def f(e):
                ins = None
                for c in range(8):
                    ins = e.matmul(pbank[bk][:, :], lhsT=onesb[:, :], rhs=sqb[:, c * 512:(c + 1) * 512], start=(c == 0), stop=(c == 7))
                return ins
            T.add("pe", f, reads=[("sqb", c) for c in range(8)] + ["onesb"], writes=[("pb", bk)])
            T.add("act", lambda e: e.activation(out=rsb, in_=pbank[bk][:, :], func=AF.Ln, bias=vec[:, V_EPS:V_EPS + 1], scale=1.0 / D),
                  reads=[("pb", bk), "vec"], writes=["rsb0", "rsb"])
            T.add("act", lambda e: e.activation(out=rsb, in_=rsb, func=AF.Exp, scale=-0.5), reads=["rsb0"], writes=["rsb"])
            for c in range(8):
                o = hnT[:, c * 512:(c + 1) * 512] if out_bf else hT[:, c * 512:(c + 1) * 512]
                T.add("dve", lambda e, c=c, o=o: e.scalar_tensor_tensor(out=o, in0=hT[:, c * 512:(c + 1) * 512], scalar=vec[:, gcol + c:gcol + c + 1],
                                                                       in1=rsb, op0=ALU.mult, op1=ALU.mult),
                      reads=[("hT", c), "rsb", "vec"], writes=[("hnT", c)] if out_bf else [("hT", c)])

        load_xC(0)
        cvc = [0]
        for s in range(NSPAN):
            t0 = s * SP
            load_p(s)
            for c in range(8):
                sl = ring_load(s_out[c], 1024, [("s_out", c)])
                bk = next_bank()

                def f(e, c=c, sl=sl, bk=bk, t0=t0):
                    o = pbank[bk]
                    for b in range(4):
                        e.matmul(o[:, b * 128:(b + 1) * 128], lhsT=xsta[:, b * D + c * 128: b * D + (c + 1) * 128], rhs=identf[:, :],
                                 start=(b == 0), stop=False, skip_group_check=True)
                    ins = None
                    for kc in range(8):
                        srct = QTa if kc < 4 else PTa
                        rhs = srct[:, (kc % 4) * S + t0:(kc % 4) * S + t0 + SP]
                        ins = e.matmul(o[:, :], lhsT=ring[sl][:, kc * 128:(kc + 1) * 128], rhs=rhs, start=False, stop=(kc == 7), skip_group_check=True)
                    return ins
                T.add("pe", f, reads=[("xs", b) for b in range(4)] + ["identf", ("ring", sl)] +
                      [("att", cc, hh, s // 4) for cc in range(4) for hh in range(2)] + [("poolT", g, s) for g in range(4)],
                      writes=[("pb", bk)])
                T.add("act", lambda e, c=c, bk=bk: e.activation(out=hT[:, c * 512:(c + 1) * 512], in_=pbank[bk][:, :], func=AF.Copy),
                      reads=[("pb", bk)], writes=[("hT", c)])
            if s + 1 < NSPAN:
                load_xC(s + 1)
            rmsnorm(V_LNFFN, True)
            hk = [("hnT", c) for c in range(8)]
            NB0 = 3
            first = []
            for j in range(NB0):
                sl = ring_load(s_up[j], 2048, [("s_up", j, 0), ("s_up", j, 1)])
                first.append((sl, next_bank(), next_bank()))
            for kc in range(8):
                def f(e, kc=kc, first=first):
                    ins = None
                    for (sl, bg, bv) in first:
                        for h2, bk in ((0, bg), (1, bv)):
                            ins = e.matmul(pbank[bk][:, :], lhsT=ring[sl][:, kc * 256 + h2 * 128: kc * 256 + (h2 + 1) * 128],
                                           rhs=hnT[:, kc * 512:(kc + 1) * 512], start=(kc == 0), stop=(kc == 7), skip_group_check=True)
                    return ins
                T.add("pe", f, reads=[("hnT", kc)] + [("ring", sl) for (sl, _, _) in first],
                      writes=[("pb", b_) for (_, bg, bv) in first for b_ in (bg, bv)])
            for j in range(NJ):
                if j < NB0:
                    sl, bg, bv = first[j]
                else:
                    sl = ring_load(s_up[j], 2048, [("s_up", j, 0), ("s_up", j, 1)])
                    bg, bv = next_bank(), next_bank()
                    for h2, bk in ((0, bg), (1, bv)):
                        def f(e, h2=h2, bk=bk, sl=sl):
                            ins = None
                            for kc in range(8):
                                ins = e.matmul(pbank[bk][:, :], lhsT=ring[sl][:, kc * 256 + h2 * 128: kc * 256 + (h2 + 1) * 128],
                                               rhs=hnT[:, kc * 512:(kc + 1) * 512], start=(kc == 0), stop=(kc == 7))
                            return ins
                        T.add("pe", f, reads=hk + [("ring", sl)], writes=[("pb", bk)])
                par = cvc[0] % 2
                cvc[0] += 1
                Gs, Vs, Yg, Yv = cv[par * 2], cv[par * 2 + 1], cv[4 + par], cv[6 + par]
                kG, kV, kYg, kYv = ("cv", par * 2), ("cv", par * 2 + 1), ("cv", 4 + par), ("cv", 6 + par)
                GV3 = arf[:, 6656 + par * 1028: 6656 + (par + 1) * 1028].rearrange("p (a b) -> p a b", a=2)
                CG3 = mk_ap(CG, (0, 1), 128, 2 * j, [[2 * NJ, 2], [1, 2]])
                T.add("pool", lambda e, GV3=GV3, CG3=CG3: e.tensor_copy(out=GV3[:, :, 0:2], in_=CG3),
                      reads=["CG", ("CGc", j)], writes=[(kG, "halo"), (kV, "halo")])
                for h2, bk, Xs, kX in ((0, bg, Gs, kG), (1, bv, Vs, kV)):
                    T.add("act", lambda e, Xs=Xs, bk=bk: e.activation(out=Xs[:, 2:514], in_=pbank[bk][:, :], func=AF.Copy),
                          reads=[("pb", bk)], writes=[kX])
                T.add("pool", lambda e, GV3=GV3, CG3=CG3: e.tensor_copy(out=CG3, in_=GV3[:, :, 512:514]),
                      reads=[kG, kV], writes=[("CGc", j)])
                for tap in range(3):
                    for h2, Xs, kX, Y, kY in ((0, Gs, kG, Yg, kYg), (1, Vs, kV, Yv, kYv)):
                        f_idx = h2 * NJ + j
                        wcol = V_CW + (2 - tap) * 44 + f_idx
                        src = Xs[:, 2 - tap: 514 - tap]
                        if tap == 0 and h2 == 0:
                            T.add("dve", lambda e, src=src, Y=Y, wcol=wcol, f_idx=f_idx: e.tensor_scalar(
                                out=Y[:, 0:512], in0=src, scalar1=vec[:, wcol:wcol + 1], scalar2=vec[:, V_CB + f_idx:V_CB + f_idx + 1],
                                op0=ALU.mult, op1=ALU.add), reads=[kX, (kX, "halo"), "vec"], writes=[kY])
                        elif tap == 0:
                            bkx = bg if h2 == 0 else bv
                            T.add("act", lambda e, bkx=bkx, Y=Y, wcol=wcol, f_idx=f_idx: e.activation(
                                out=Y[:, 0:512], in_=pbank[bkx][:, :], func=AF.Identity, bias=vec[:, V_CB + f_idx:V_CB + f_idx + 1],
                                scale=vec[:, wcol:wcol + 1]), reads=[("pb", bkx), "vec"], writes=[kY])
                        else:
                            T.add("dve", lambda e, src=src, Y=Y, wcol=wcol: e.scalar_tensor_tensor(
                                out=Y[:, 0:512], in0=src, scalar=vec[:, wcol:wcol + 1], in1=Y[:, 0:512], op0=ALU.mult, op1=ALU.add),
                                reads=[kX, (kX, "halo"), kY, "vec"], writes=[kY])
                T.add("act", lambda e, Yg=Yg: e.activation(out=Yg[:, 0:512], in_=Yg[:, 0:512], func=AF.Silu), reads=[kYg], writes=[kYg])
                T.add("pool", lambda e, Yg=Yg, Yv=Yv, j=j: e.tensor_tensor(out=actT[:, j * 512:(j + 1) * 512], in0=Yg[:, 0:512], in1=Yv[:, 0:512], op=ALU.mult),
                      reads=[kYg, kYv], writes=[("actT", j)])
            ak = [("actT", j) for j in range(NJ)]
            ND0 = 3
            firstd = []
            for c in range(ND0):
                sl = ring_load(s_down[c], NJ * 128, [("s_down", c)])
                firstd.append((sl, next_bank()))
            for kc in range(NJ):
                def f(e, kc=kc, firstd=firstd):
                    ins = None
                    for (sl, bk) in firstd:
                        ins = e.matmul(pbank[bk][:, :], lhsT=ring[sl][:, kc * 128:(kc + 1) * 128], rhs=actT[:, kc * 512:(kc + 1) * 512],
                                       start=(kc == 0), stop=(kc == NJ - 1), skip_group_check=True)
                    return ins
                T.add("pe", f, reads=[("actT", kc)] + [("ring", sl) for (sl, _) in firstd], writes=[("pb", bk) for (_, bk) in firstd])
            for c in range(8):
                if c < ND0:
                    bk = firstd[c][1]
                else:
                    sl = ring_load(s_down[c], NJ * 128, [("s_down", c)])
                    bk = next_bank()

                    def f(e, sl=sl, bk=bk):
                        ins = None
                        for kc in range(NJ):
                            ins = e.matmul(pbank[bk][:, :], lhsT=ring[sl][:, kc * 128:(kc + 1) * 128], rhs=actT[:, kc * 512:(kc + 1) * 512],
                                           start=(kc == 0), stop=(kc == NJ - 1))
                        return ins
                    T.add("pe", f, reads=ak + [("ring", sl)], writes=[("pb", bk)])
                T.add("dve", lambda e, c=c, bk=bk: e.tensor_tensor(out=hT[:, c * 512:(c + 1) * 512], in0=pbank[bk][:, :], in1=hT[:, c * 512:(c + 1) * 512], op=ALU.add),
                      reads=[("pb", bk), ("hT", c)], writes=[("hT", c)])
            for kc in range(2):
                bk = next_bank()

                def f(e, kc=kc, bk=bk):
                    ins = None
                    for b in range(4):
                        ins = e.matmul(pbank[bk][:, b * 128:(b + 1) * 128], lhsT=pst[:, b * 256 + kc * 128: b * 256 + (kc + 1) * 128], rhs=identf[:, :],
                                       start=(b == 0), stop=(b == 3), skip_group_check=True)
                    return ins
                T.add("pe", f, reads=[("pst", b) for b in range(4)] + ["identf"], writes=[("pb", bk)])
                T.add("act", lambda e, kc=kc, bk=bk: e.activation(out=pT[:, kc * 512:(kc + 1) * 512], in_=pbank[bk][:, :], func=AF.Copy),
                      reads=[("pb", bk)], writes=[("pT", kc)])

            def ple_mm(c):
                slp = ring_load(s_ple[c], 256, [("s_ple", c)])
                bp = next_bank()

                def fp(e):
                    ins = None
                    for kc in range(2):
                        ins = e.matmul(pbank[bp][:, :], lhsT=ring[slp][:, kc * 128:(kc + 1) * 128], rhs=pT[:, kc * 512:(kc + 1) * 512],
                                       start=(kc == 0), stop=(kc == 1))
                    return ins
                T.add("pe", fp, reads=[("pT", 0), ("pT", 1), ("ring", slp)], writes=[("pb", bp)])
                return bp
            NG0 = 3
            bps = [ple_mm(c) for c in range(NG0)]
            rmsnorm(V_LNPLE, True, avoid=bps)
            firstg = []
            for c in range(NG0):
                slg = ring_load(s_gate[c], 1024, [("s_gate", c)])
                firstg.append((slg, next_bank(avoid=bps)))
            for kc in range(8):
                def f(e, kc=kc, firstg=firstg):
                    ins = None
                    for (slg, bg) in firstg:
                        ins = e.matmul(pbank[bg][:, :], lhsT=ring[slg][:, kc * 128:(kc + 1) * 128], rhs=hnT[:, kc * 512:(kc + 1) * 512],
                                       start=(kc == 0), stop=(kc == 7), skip_group_check=True)
                    return ins
                T.add("pe", f, reads=[("hnT", kc)] + [("ring", slg) for (slg, _) in firstg], writes=[("pb", bg) for (_, bg) in firstg])
            for c in range(8):
                if c < NG0:
                    bg, bp = firstg[c][1], bps[c]
                else:
                    slg = ring_load(s_gate[c], 1024, [("s_gate", c)])
                    bg = next_bank()

                    def fg(e, slg=slg, bg=bg):
                        ins = None
                        for kc in range(8):
                            ins = e.matmul(pbank[bg][:, :], lhsT=ring[slg][:, kc * 128:(kc + 1) * 128], rhs=hnT[:, kc * 512:(kc + 1) * 512],
                                           start=(kc == 0), stop=(kc == 7))
                        return ins
                    T.add("pe", fg, reads=hk + [("ring", slg)], writes=[("pb", bg)])
                    bp = ple_mm(c)
                sg = sgt[c % 2]
                ksg = ("cv", 4 + c % 2)
                T.add("act", lambda e, sg=sg, bg=bg: e.activation(out=sg[:, 0:512], in_=pbank[bg][:, :], func=AF.Sigmoid),
                      reads=[("pb", bg)], writes=[ksg])
                T.add("dve", lambda e, sg=sg, bp=bp: e.tensor_tensor(out=sg[:, 0:512], in0=sg[:, 0:512], in1=pbank[bp][:, :], op=ALU.mult),
                      reads=[ksg, ("pb", bp)], writes=[ksg])
                T.add("dve", lambda e, sg=sg, c=c: e.tensor_tensor(out=hT[:, c * 512:(c + 1) * 512], in0=sg[:, 0:512], in1=hT[:, c * 512:(c + 1) * 512], op=ALU.add),
                      reads=[ksg, ("hT", c)], writes=[("hT", c)])
            rmsnorm(V_LNFIN, False)
            for b in range(4):
                gb = 4 * s + b
                osl = gb % 2

                def f(e, b=b):
                    ins = None
                    for c in range(8):
                        ins = e.transpose(out=pbig[:, c * 128:(c + 1) * 128], in_=hT[:, c * 512 + b * 128: c * 512 + (b + 1) * 128], identity=identf[:, :])
                    return ins
                T.add("pe", f, reads=[("hT", c) for c in range(8)] + ["identf"], writes=["pbig"])
                if b % 2 == 0:
                    T.add("act", lambda e, osl=osl: e.activation(out=ost[osl], in_=pbig[:, :], func=AF.Copy), reads=["pbig"], writes=[("ost", osl)])
                else:
                    T.add("dve", lambda e, osl=osl: e.tensor_copy(out=ost[osl], in_=pbig[:, :]), reads=["pbig"], writes=[("ost", osl)])
                T.add("pool", lambda e, osl=osl, gb=gb: e.dma_start(out=out[gb * 128:(gb + 1) * 128, :], in_=ost[osl]),
                      reads=[("ost", osl)], writes=[("outd", osl)], dma=True, semkey=("outd", osl))
        T.add("sp", None, reads=[("outd", 0), ("outd", 1)])

        with nc.Block() as block:
            T.emit(nc, block, sems, new_sem)
    return nc


def _consts():
    ident = np.eye(128, dtype=np.float32)
    k = np.arange(128)[:, None].astype(np.float64)
    q = np.arange(128)[None, :].astype(np.float64)
    tabs = []
    for e in range(12):
        a = 2.0 ** (e - 8)
        cur = np.where(k <= q, np.exp(-a * (q - k)), 0.0)
        prev = np.where(k >= q, np.exp(-a * (q + 128 - k)), 0.0)
        tabs.append(np.concatenate([cur, prev], axis=1))
    bias_tab = np.concatenate(tabs, axis=1).astype(np.float32)
    return ident, bias_tab


def _vecs(ln_mix, ln_ffn, ln_ple, ln_final, pool_scale, conv_w, conv_b):
    v = np.zeros((128, NV), dtype=np.float32)
    v[:, V_LNMIX:V_LNMIX + 8] = ln_mix.reshape(8, 128).T
    v[:, V_LNFFN:V_LNFFN + 8] = ln_ffn.reshape(8, 128).T
    v[:, V_LNPLE:V_LNPLE + 8] = ln_ple.reshape(8, 128).T
    v[:, V_LNFIN:V_LNFIN + 8] = ln_final.reshape(8, 128).T
    v[:, V_PSC:V_PSC + 4] = pool_scale.reshape(4, 128).T
    for kk in range(3):
        v[:, V_CW + kk * 44: V_CW + (kk + 1) * 44] = conv_w[kk].reshape(44, 128).T
    v[:, V_CB:V_CB + 44] = conv_b.reshape(44, 128).T
    v[:, V_CNT:V_CNT + 16] = (1.0 / np.arange(1, 17, dtype=np.float32))[None, :]
    v[:, V_EPS] = EPS
    return v


def kernel(x, p, ln_mix, w_in, pool_w, pool_scale, w_out, ln_ffn, w_up, conv_w, conv_b,
           w_down, ln_ple, w_ple_gate, w_ple, ln_final):
    f = lambda a: np.ascontiguousarray(np.asarray(a, dtype=np.float32))
    x, p = f(x), f(p)
    ident, bias_tab = _consts()
    vecs = _vecs(f(ln_mix)[0], f(ln_ffn)[0], f(ln_ple)[0], f(ln_final), f(pool_scale)[0], f(conv_w)[0], f(conv_b)[0])
    shared = {"w_in": f(w_in)[0], "w_out": f(w_out)[0], "w_up": f(w_up)[0], "w_down": f(w_down)[0], "w_gate": f(w_ple_gate)[0],
              "w_ple": f(w_ple)[0], "pool_w": f(pool_w)[0], "vecs": vecs, "ident": ident, "bias_tab": bias_tab}
    nc = build_program()
    in_maps = []
    for b in range(8):
        m = dict(shared)
        m["x"] = np.ascontiguousarray(x[b])
        m["p"] = np.ascontiguousarray(p[0, b])
        in_maps.append(m)
    res = run_bass_kernel_spmd(nc, in_maps, core_ids=list(range(8)))
    return np.stack([np.asarray(r["out"], dtype=np.float32) for r in res.results], axis=0)
```

```python
import numpy as np
import concourse.bass as bass
import concourse.mybir as mybir
from concourse.bass_utils import run_bass_kernel_spmd

F32 = mybir.dt.float32
BF16 = mybir.dt.bfloat16
AF = mybir.ActivationFunctionType
ALU = mybir.AluOpType

S = 4096
D = 1024
NBLK = 32
NSPAN = 8
SP = 512
DFF = 2816
NJ = 22
PLE = 256
EPS = 1e-6
NEG = -30000.0

V_LNMIX, V_LNFFN, V_LNPLE, V_LNFIN, V_PSC, V_CW, V_CB, V_CNT, V_EPS, NV = 0, 8, 16, 24, 32, 36, 168, 212, 228, 229


class Tracker:
    ENG = ("pe", "act", "dve", "pool", "sp")

    def __init__(self):
        self.ops = []
        self.last_w = {}
        self.readers = {}
        self.pending = {}
        self.dma_since = []
        self.last_on = {}

    def add(self, eng, fn, reads=(), writes=(), dma=False, semkey=None):
        idx = len(self.ops)
        raw, other = set(), set()
        for k in reads:
            w = self.last_w.get(k)
            if w is not None:
                raw.add(w)
        for k in writes:
            w = self.last_w.get(k)
            if w is not None:
                other.add(w)
            other.update(self.readers.get(k, ()))
        if eng in self.pending:
            raw.update(self.pending.pop(eng))
        other -= raw
        self.ops.append(dict(eng=eng, fn=fn, raw=raw, other=other, dma=dma, semkey=semkey, idx=idx))
        for k in reads:
            self.readers.setdefault(k, []).append(idx)
        for k in writes:
            self.last_w[k] = idx
            self.readers[k] = []
        self.last_on[eng] = idx
        if dma:
            self.dma_since.append(idx)
        return idx

    def barrier(self):
        snap = set(self.last_on.values()) | set(self.dma_since)
        self.dma_since = []
        for e in self.ENG:
            self.pending[e] = set(snap) | self.pending.get(e, set())

    def emit(self, nc, block, sems, new_sem):
        ops = self.ops
        for op in ops:
            deps = set()
            for d in op["raw"] | op["other"]:
                dop = ops[d]
                if dop["fn"] is None:
                    continue
                if (not dop["dma"]) and dop["eng"] == op["eng"]:
                    if op["eng"] == "pe" or d not in op["raw"]:
                        continue
                deps.add(d)
            op["deps"] = deps
        marked = set()
        for op in ops:
            marked.update(op["deps"])
        cnt = {e: 0 for e in self.ENG}
        dcnt = {}
        dsem = {}
        for op in ops:
            if op["dma"]:
                k = (op["semkey"], op["eng"])
                dcnt[k] = dcnt.get(k, 0) + 16
                op["val"] = dcnt[k]
                if k not in dsem:
                    dsem[k] = new_sem("d%d" % len(dsem))
                op["sem"] = dsem[k]
            elif op["idx"] in marked:
                cnt[op["eng"]] += 1
                op["val"] = cnt[op["eng"]]
                op["sem"] = sems[op["eng"]]
        per_eng = {e: [op for op in ops if op["eng"] == e] for e in self.ENG}
        know = {e: {} for e in self.ENG}
        semobj = {}
        for op in ops:
            e = op["eng"]
            kn = know[e]
            waits = []
            for d in sorted(op["deps"]):
                dop = ops[d]
                sid = id(dop["sem"])
                semobj[sid] = dop["sem"]
                if kn.get(sid, 0) >= dop["val"]:
                    continue
                waits.append((dop["sem"], dop["val"]))
                for k2, v2 in dop["vc"].items():
                    if kn.get(k2, 0) < v2:
                        kn[k2] = v2
            wmax = {}
            for s_, v_ in waits:
                if wmax.get(id(s_), (None, 0))[1] < v_:
                    wmax[id(s_)] = (s_, v_)
            op["waits"] = list(wmax.values())
            vc = dict(kn)
            if "val" in op:
                sid = id(op["sem"])
                if vc.get(sid, 0) < op["val"]:
                    vc[sid] = op["val"]
                if not op["dma"]:
                    kn[sid] = max(kn.get(sid, 0), 0)
            op["vc"] = vc

        def run(engname, eh):
            for op in per_eng[engname]:
                for (s, v) in op["waits"]:
                    eh.wait_ge(s, v)
                if op["fn"] is None:
                    continue
                ins = op["fn"](eh)
                if op["dma"]:
                    ins.then_inc(op["sem"], 16)
                elif op["idx"] in marked:
                    ins.then_inc(op["sem"], 1)

        @block.tensor
        def _(e):
            run("pe", e)

        @block.scalar
        def _(e):
            run("act", e)

        @block.vector
        def _(e):
            run("dve", e)

        @block.gpsimd
        def _(e):
            run("pool", e)

        @block.sync
        def _(e):
            run("sp", e)


def build_program():
    nc = bass.Bass("TRN2", target_bir_lowering=False)
    dt_in = lambda n, shp: nc.dram_tensor(n, shp, F32, kind="ExternalInput").ap()
    x = dt_in("x", [S, D])
    p = dt_in("p", [S, PLE])
    w_in = dt_in("w_in", [D, 2048])
    w_out = dt_in("w_out", [D, D])
    w_up = dt_in("w_up", [D, 2 * DFF])
    w_down = dt_in("w_down", [DFF, D])
    w_gate = dt_in("w_gate", [D, D])
    w_ple = dt_in("w_ple", [PLE, D])
    pool_w = dt_in("pool_w", [4, 128, 128])
    vecs = dt_in("vecs", [128, NV])
    ident = dt_in("ident", [128, 128])
    bias_tab = dt_in("bias_tab", [128, 12 * 256])
    out = nc.dram_tensor("out", [S, D], F32, kind="ExternalOutput").ap()
    s_in = nc.dram_tensor("s_in", [12, 128, 8 * 128], BF16, kind="Internal").ap()
    s_out = nc.dram_tensor("s_out", [8, 128, 8 * 128], BF16, kind="Internal").ap()
    s_up = nc.dram_tensor("s_up", [NJ, 128, 8 * 2 * 128], BF16, kind="Internal").ap()
    s_down = nc.dram_tensor("s_down", [8, 128, NJ * 128], BF16, kind="Internal").ap()
    s_gate = nc.dram_tensor("s_gate", [8, 128, 8 * 128], BF16, kind="Internal").ap()
    s_ple = nc.dram_tensor("s_ple", [8, 128, 2 * 128], BF16, kind="Internal").ap()

    T = Tracker()
    from contextlib import ExitStack
    es = ExitStack()
    with es:
        sb = lambda n, shp, d: es.enter_context(nc.sbuf_tensor(n, shp, d))
        ps = lambda n, shp, d: es.enter_context(nc.psum_tensor(n, shp, d))
        QT = sb("QT", [128, 4 * S], BF16)
        KT = sb("KT", [128, 4 * S], BF16)
        VLEN = 64 + NBLK * 512 + 64
        V1 = sb("V1", [128, VLEN], BF16)
        PT_ = sb("poolT", [128, 4 * S], BF16)
        xst = sb("xst", [128, 4 * D], F32)
        identf = sb("identf", [128, 128], F32)[:, :]
        identb = sb("identb", [128, 128], BF16)[:, :]
        onesb = sb("onesb", [128, 128], BF16)[:, :]
        vec = sb("vec", [128, NV], F32)[:, :]
        AR_N = 13312
        arf_t = sb("arena", [128, AR_N], F32)
        arf = arf_t[:, :]
        arb = arf.bitcast(BF16)
        QTa, KTa, V1a, xsta, PTa = QT[:, :], KT[:, :], V1[:, :], xst[:, :], PT_[:, :]
        ringA = [arb[:, i * 1024:(i + 1) * 1024] for i in range(4)]
        wv = arb[:, 4096:8192]
        xnb = [arb[:, 8192 + i * 1024: 8192 + (i + 1) * 1024] for i in range(2)]
        junk = arb[:, 10240:11264]
        xnT = arb[:, 11264:15360]
        Gb = arb[:, 15360:16384]
        dlt = arb[:, 16384:18432]
        poolw = arb[:, 18432:18944]
        Ut = [arf[:, 9472 + g * 528: 9472 + (g + 1) * 528] for g in range(4)]
        St = [arf[:, 11584 + i * 528: 11584 + (i + 1) * 528] for i in range(3)]
        ss = arf[:, 13168:13200]
        sq_ = arf[:, 13200:13232]
        rstd = arf[:, 13232:13264]
        V4LEN = 64 + 32 * 128 + 64
        V16LEN = 64 + 32 * 256 + 64
        V4c = arb[:, 0:V4LEN]
        V16 = arb[:, 4224:4224 + V16LEN]
        PTg = [arb[:, 12544 + g * 1024: 12544 + (g + 1) * 1024] for g in range(2)]
        PTr = [PTg[i // 4][:, (i % 4) * 256:(i % 4 + 1) * 256] for i in range(8)]
        biasb = arb[:, 14592:14592 + 3072]
        acc = [arf[:, 8832 + i * 2048: 8832 + (i + 1) * 2048] for i in range(2)]
        rden = xsta[:, 0:2048]
        sqb = arb[:, 0:4096]
        hT = arf[:, 2048:6144]
        rsb = arf[:, 6144:6656]
        cv = [arf[:, 6656 + i * 514: 6656 + (i + 1) * 514] for i in range(8)]
        ost = [arf[:, 10768 + i * 1024: 10768 + (i + 1) * 1024] for i in range(2)]
        CG = arf[:, 12816:12816 + 176]
        sgt = [cv[4], cv[5]]
        actT = KTa[:, 0:NJ * 512]
        hnT = KTa[:, 11264:15360]
        pT = KTa[:, 15360:16384]
        RSL = 2816
        ring = [V1a[:, i * RSL:(i + 1) * RSL] for i in range(5)]
        pst = V1a[:, 14080:16128].bitcast(F32)
        pbank = [ps("pb%d" % i, [128, 512], F32) for i in range(6)]
        pbig = ps("pbig", [128, 1024], F32)
        pbig_b = pbig[:, :].bitcast(BF16)

        sems = {e: es.enter_context(nc.semaphore("s_" + e)) for e in Tracker.ENG}
        new_sem = lambda n: es.enter_context(nc.semaphore(n))

        pbc = [0]

        def next_bank(avoid=()):
            while True:
                i = pbc[0] % 6
                pbc[0] += 1
                if i not in avoid:
                    return i

        def v3(ap, a):
            return ap.rearrange("p (a b) -> p a b", a=a)

        T.add("sp", lambda e: e.dma_start(out=identf[:, :], in_=ident[:, :]), writes=["identf"], dma=True, semkey="setup")
        T.add("sp", lambda e: e.dma_start(out=vec[:, :], in_=vecs[:, :]), writes=["vec"], dma=True, semkey="setup")
        T.add("pool", lambda e: e.dma_start(out=identb[:, :], in_=ident[:, :]), writes=["identb"], dma=True, semkey="setup")
        T.add("pool", lambda e: e.dma_start(out=poolw.rearrange("c (g e) -> c g e", g=4),
                                            in_=pool_w.rearrange("g c e -> c g e")), writes=["poolw"], dma=True, semkey="setup")
        T.add("pool", lambda e: e.dma_start(out=wv.rearrange("p (k n) -> p k n", k=8),
                                            in_=w_in[:, 1024:1536].rearrange("(k p) n -> p k n", p=128)), writes=["wv"], dma=True, semkey="setup")
        for j in range(12):
            col0 = (j if j < 8 else j + 4) * 128
            T.add("pool", lambda e, j=j, col0=col0: e.dma_start(out=s_in[j].rearrange("p (k n) -> p k n", k=8),
                                                                in_=w_in[:, col0:col0 + 128].rearrange("(k p) n -> p k n", p=128)),
                  writes=[("s_in", j)], dma=True, semkey="setup")
        T.add("dve", lambda e: e.memset(onesb[:, :], 1.0), writes=["onesb"])
        T.add("dve", lambda e: e.memset(V1a[:, 0:64], 1.0), writes=["V1ones"])
        T.add("dve", lambda e: e.memset(V1a[:, VLEN - 64:VLEN], 1.0), writes=["V1ones2"])
        T.add("dve", lambda e: e.memset(Gb, 1.0), writes=["Gb"])
        for g in range(4):
            T.add("pool", lambda e, g=g: e.memset(Ut[g][:, 0:16], 0.0), writes=[("U", g)])
        for i in range(3):
            T.add("pool", lambda e, i=i: e.memset(St[i], 0.0), writes=[("St", i)])
        T.barrier()
        for c in range(8):
            T.add("dve", lambda e, c=c: e.tensor_scalar(out=Gb[:, c * 128:(c + 1) * 128], in0=Gb[:, c * 128:(c + 1) * 128],
                                                        scalar1=vec[:, V_LNMIX + c:V_LNMIX + c + 1], scalar2=None, op0=ALU.mult),
                  reads=["vec"], writes=["Gb"])
        prep = []
        for j in range(8):
            prep.append((lambda e, j=j: e.dma_start(out=s_out[j].rearrange("p (k n) -> p k n", k=8),
                                                    in_=w_out[:, j * 128:(j + 1) * 128].rearrange("(k p) n -> p k n", p=128)), ("s_out", j)))
        for j in range(NJ):
            for h in range(2):
                prep.append((lambda e, j=j, h=h: e.dma_start(
                    out=s_up[j].rearrange("p (k h n) -> p k h n", k=8, h=2)[:, :, h, :],
                    in_=w_up[:, h * DFF + j * 128: h * DFF + (j + 1) * 128].rearrange("(k p) n -> p k n", p=128)), ("s_up", j, h)))
        for j in range(8):
            prep.append((lambda e, j=j: e.dma_start(out=s_down[j].rearrange("p (k n) -> p k n", k=NJ),
                                                    in_=w_down[:, j * 128:(j + 1) * 128].rearrange("(k p) n -> p k n", p=128)), ("s_down", j)))
        for j in range(8):
            prep.append((lambda e, j=j: e.dma_start(out=s_gate[j].rearrange("p (k n) -> p k n", k=8),
                                                    in_=w_gate[:, j * 128:(j + 1) * 128].rearrange("(k p) n -> p k n", p=128)), ("s_gate", j)))
            prep.append((lambda e, j=j: e.dma_start(out=s_ple[j].rearrange("p (k n) -> p k n", k=2),
                                                    in_=w_ple[:, j * 128:(j + 1) * 128].rearrange("(k p) n -> p k n", p=128)), ("s_ple", j)))

        def issue_prep(nmax):
            for _ in range(nmax):
                if prep:
                    fn, key = prep.pop(0)
                    T.add("pool", fn, writes=[key], dma=True, semkey="prep")

        xnTs = [xnT, xsta[:, 2048:4096].bitcast(BF16)]

        def load_x(gb):
            sl = gb % 2
            T.add("pool", lambda e: e.dma_start(out=xst[:, sl * D:(sl + 1) * D], in_=x[gb * 128:(gb + 1) * 128, :]),
                  writes=[("xs", sl)], dma=True, semkey=("xs", sl))

        for gb in range(2):
            load_x(gb)
        rca = [0]

        def ringA_load(j):
            sl = rca[0] % 4
            rca[0] += 1
            T.add("sp", lambda e: e.dma_start(out=ringA[sl], in_=s_in[j]), reads=[("s_in", j)], writes=[("ringA", sl)],
                  dma=True, semkey=("ringA", sl))
            return sl

        def norm_block(s, b):
            gb = 4 * s + b
            sl = gb % 2
            xs = xst[:, sl * D:(sl + 1) * D]
            xb = xnb[gb % 2]
            xt = xnTs[s % 2]
            T.add("act", lambda e: e.activation(out=junk, in_=xs, func=AF.Square, accum_out=ss[:, gb:gb + 1]),
                  reads=[("xs", sl)], writes=["junk", ("ss", gb)])
            T.add("act", lambda e: e.activation(out=sq_[:, gb:gb + 1], in_=ss[:, gb:gb + 1], func=AF.Sqrt,
                                                bias=vec[:, V_EPS:V_EPS + 1], scale=1.0 / D),
                  reads=[("ss", gb), "vec"], writes=[("sq", gb)])
            T.add("dve", lambda e: e.reciprocal(out=rstd[:, gb:gb + 1], in_=sq_[:, gb:gb + 1]),
                  reads=[("sq", gb)], writes=[("rstd", gb)])
            T.add("dve", lambda e: e.tensor_scalar(out=xb, in0=xs, scalar1=rstd[:, gb:gb + 1], scalar2=None, op0=ALU.mult),
                  reads=[("xs", sl), ("rstd", gb)], writes=[("xnb", gb % 2)])

            def tr(e):
                ins = None
                for c in range(8):
                    ins = e.transpose(out=pbig_b[:, c * 128:(c + 1) * 128], in_=xb[:, c * 128:(c + 1) * 128], identity=identb[:, :])
                return ins
            T.add("pe", tr, reads=[("xnb", gb % 2), "identb"], writes=["pbig"])
            T.add("dve", lambda e: e.tensor_tensor(out=v3(xt, 8)[:, :, b * 128:(b + 1) * 128], in0=v3(pbig_b[:, 0:1024], 8),
                                                   in1=v3(Gb, 8), op=ALU.mult),
                  reads=["pbig", "Gb"], writes=[("xnT", s % 2, b)])
            if gb + 2 < NBLK:
                load_x(gb + 2)

        def proj_task(s, t):
            xt = xnTs[s % 2]
            xk = [("xnT", s % 2, b) for b in range(4)]
            if t < 12:
                j = t
                rsl = ringA_load(j)
                bk = next_bank()

                def mm(e):
                    ins = None
                    for kc in range(8):
                        ins = e.matmul(pbank[bk][:, :], lhsT=ringA[rsl][:, kc * 128:(kc + 1) * 128],
                                       rhs=xt[:, kc * 512:(kc + 1) * 512], start=(kc == 0), stop=(kc == 7))
                    return ins
                T.add("pe", mm, reads=xk + [("ringA", rsl)], writes=[("pb", bk)])
                if j < 4:
                    T.add("act", lambda e: e.activation(out=QT[:, j * S + s * SP: j * S + (s + 1) * SP], in_=pbank[bk][:, :],
                                                        func=AF.Copy, scale=0.125),
                          reads=[("pb", bk)], writes=[("QT", j, s, 0), ("QT", j, s, 1)])
                elif j < 8:
                    c = j - 4
                    T.add("dve", lambda e: e.tensor_copy(out=KT[:, c * S + s * SP: c * S + (s + 1) * SP], in_=pbank[bk][:, :]),
                          reads=[("pb", bk)], writes=[("KT", c, s)])
                else:
                    g = j - 8
                    T.add("act", lambda e: e.activation(out=Ut[g][:, 16:528], in_=pbank[bk][:, :], func=AF.Copy),
                          reads=[("pb", bk)], writes=[("U", g)])
                    src = Ut[g]
                    for lvl in range(g + 1):
                        sh = 1 << lvl
                        dst = St[lvl % 2]
                        T.add("dve", lambda e, src=src, dst=dst, sh=sh: e.tensor_tensor(out=dst[:, sh:528], in0=src[:, sh:528], in1=src[:, 0:528 - sh], op=ALU.add),
                              reads=[("U", g), ("St", (lvl - 1) % 2)] if lvl else [("U", g)], writes=[("St", lvl % 2)])
                        src = dst
                    w = 2 << g
                    kst = ("St", g % 2)
                    T.add("dve", lambda e: e.scalar_tensor_tensor(out=dlt[:, g * 512:(g + 1) * 512], in0=src[:, 16:528], scalar=1.0 / w,
                                                                  in1=Ut[g][:, 16:528], op0=ALU.mult, op1=ALU.subtract),
                          reads=[("U", g), kst], writes=[("dlt", g)])
                    if s == 0:
                        T.add("dve", lambda e: e.tensor_tensor(out=src[:, 16:16 + w - 1], in0=src[:, 16:16 + w - 1],
                                                               in1=vec[:, V_CNT:V_CNT + w - 1], op=ALU.mult),
                              reads=[kst, ("dlt", g), "vec"], writes=[kst])
                        T.add("dve", lambda e: e.tensor_tensor(out=dlt[:, g * 512: g * 512 + w - 1], in0=src[:, 16:16 + w - 1],
                                                               in1=Ut[g][:, 16:16 + w - 1], op=ALU.subtract),
                              reads=[kst, ("U", g)], writes=[("dlt", g)])
                    T.add("dve", lambda e: e.tensor_copy(out=Ut[g][:, 0:16], in_=Ut[g][:, 512:528]),
                          reads=[("U", g), ("dlt", g), ("St", 0), ("St", 1)], writes=[("U", g)])
                    bk2 = next_bank()
                    T.add("pe", lambda e: e.matmul(pbank[bk2][:, :], lhsT=poolw[:, g * 128:(g + 1) * 128],
                                                   rhs=dlt[:, g * 512:(g + 1) * 512], start=True, stop=True),
                          reads=[("dlt", g), "poolw"], writes=[("pb", bk2)])
                    T.add("act", lambda e: e.activation(out=PT_[:, g * S + s * SP: g * S + (s + 1) * SP], in_=pbank[bk2][:, :],
                                                        func=AF.Copy, scale=vec[:, V_PSC + g:V_PSC + g + 1]),
                          reads=[("pb", bk2), "vec"], writes=[("poolT", g, s)])
            else:
                b = t - 12
                gb = 4 * s + b
                bk = next_bank()

                def mmv(e):
                    ins = None
                    for kc in range(8):
                        ins = e.matmul(pbank[bk][:, :], lhsT=xt[:, kc * 512 + b * 128: kc * 512 + (b + 1) * 128],
                                       rhs=wv[:, kc * 512:(kc + 1) * 512], start=(kc == 0), stop=(kc == 7))
                    return ins
                T.add("pe", mmv, reads=xk + ["wv"], writes=[("pb", bk)])
                if b % 2 == 0:
                    T.add("act", lambda e: e.activation(out=V1[:, 64 + gb * 512: 64 + (gb + 1) * 512], in_=pbank[bk][:, :], func=AF.Copy),
                          reads=[("pb", bk)], writes=[("V1", gb)])
                else:
                    T.add("dve", lambda e: e.tensor_copy(out=V1[:, 64 + gb * 512: 64 + (gb + 1) * 512], in_=pbank[bk][:, :]),
                          reads=[("pb", bk)], writes=[("V1", gb)])

        for b in range(4):
            norm_block(0, b)
        for s in range(NSPAN):
            for t in range(16):
                proj_task(s, t)
                issue_prep(1)
                if t % 4 == 3 and s + 1 < NSPAN:
                    norm_block(s + 1, t // 4)
        issue_prep(1000)
        T.barrier()

        T.add("pool", lambda e: e.dma_start(out=biasb, in_=bias_tab[:, :]), writes=["biasb"], dma=True, semkey="setupB")
        T.add("dve", lambda e: e.memset(V4c[:, 0:64], 1.0), writes=["V4ones"])
        T.add("dve", lambda e: e.memset(V4c[:, V4LEN - 64:V4LEN], 1.0), writes=["V4ones"])
        T.add("dve", lambda e: e.memset(V16[:, 0:64], 1.0), writes=["V16ones"])
        T.add("dve", lambda e: e.memset(V16[:, V16LEN - 64:V16LEN], 1.0), writes=["V16ones"])
        v1keys = [("V1", gb) for gb in range(NBLK)]

        def mk_ap(t, part0, nparts, off, dims):
            rowlen = t.ap[0][0]
            return bass.AP(t.tensor, t.offset + part0[0] * rowlen + off, [[rowlen * part0[1], nparts]] + [list(d) for d in dims])

        def load_v4(c):
            for r in range(4):
                for a in range(4):
                    dst = mk_ap(V4c, (32 * a, 1), 32, 64 + r * 128, [[512, 8], [1, 128]])
                    src = mk_ap(V1a, (r, 4), 32, 64 + a * 512 + c * 128, [[2048, 8], [1, 128]])
                    T.add("sp", lambda e, dst=dst, src=src: e.dma_start(out=dst, in_=src), reads=v1keys, writes=[("V4c", r, a)],
                          dma=True, semkey="V4c")
        def load_v16(hf):
            for r in range(16):
                for a in range(16):
                    dst = mk_ap(V16, (8 * a, 1), 8, 64 + r * 256, [[16 * 256, 2], [1, 256]])
                    src = mk_ap(V1a, (r, 16), 8, 64 + a * 512 + hf * 256, [[16 * 512, 2], [1, 256]])
                    T.add("sp" if (a % 2 == 0) else "pool", lambda e, dst=dst, src=src: e.dma_start(out=dst, in_=src), reads=v1keys,
                          writes=[("V16", r, a)], dma=True, semkey=("V16", r, a % 2))

        def vones(t, tlen, voff, hh):
            rowlen = t.ap[0][0]
            if hh == 0:
                first, second = voff, tlen - 64
            else:
                first, second = 0, voff
            return bass.AP(t.tensor, t.offset + first, [[rowlen, 128], [second - first, 2], [1, 64]])

        stc = [0]
        ptc = [0]
        ob = [0]
        hu = [0]
        for c in range(4):
            load_v4(c)
            if c % 2 == 0:
                load_v16(c // 2)
            v4keys = [("V4c", r, a) for r in range(4) for a in range(4)]
            for n in range(2):
                for hh in range(2):
                    h = 2 * c + hh
                    asl = hu[0] % 2
                    hu[0] += 1
                    p0 = 64 * hh
                    tasks = []
                    for d in (1, 4, 16):
                        e_idx = (-(h + 1) + {1: 0, 4: 2, 16: 4}[d]) + 8
                        nper = 16 // d
                        groups = []
                        if d == 1:
                            for g4 in range(4):
                                groups.append([(0, n * 16 + g4 * 4 + i) for i in range(4)])
                        elif d == 4:
                            for jj in range(4):
                                groups.append([(r, n * 4 + jj) for r in range(4)])
                        else:
                            for g4 in range(4):
                                groups.append([(g4 * 4 + i, n) for i in range(4)])
                        for gi, grp in enumerate(groups):
                            for qi, (r, j) in enumerate(grp):
                                tasks.append(dict(d=d, r=r, j=j, e=e_idx, gi=gi, qi=qi, last=(qi == 3),
                                                  nprev=sum(1 for (_, jj_) in grp if jj_ > 0)))
                    LAG = 3
                    state = {}

                    def emit_st(t):
                        d, r, j = t["d"], t["r"], t["j"]
                        st = stc[0] % 4
                        stc[0] += 1
                        if t["qi"] == 0:
                            state["ptg"] = ptc[0] % 2
                            ptc[0] += 1
                        pt = state["ptg"] * 4 + t["qi"]
                        t["pt"] = pt
                        t["ptg"] = state["ptg"]
                        bank, half = 2 + st, 0
                        ncol = 256 if j > 0 else 128
                        t["ncol"] = ncol
                        qbase = c * S + 128 * d * j + r
                        qap = mk_ap(QTa, (p0, 1), 64, qbase, [[d, 128]])
                        kcur = mk_ap(KTa, (p0, 1), 64, qbase, [[d, 128]])
                        kprev = mk_ap(KTa, (p0, 1), 64, qbase - 128 * d, [[d, 128]]) if j > 0 else None
                        e_idx = t["e"]

                        def f(e):
                            o = pbank[bank]
                            ins = e.matmul(o[:, half:half + 128], lhsT=kcur, rhs=qap, start=True, stop=(kprev is None), skip_group_check=True)
                            if kprev is not None:
                                ins = e.matmul(o[:, half + 128:half + 256], lhsT=kprev, rhs=qap, start=False, stop=True, skip_group_check=True)
                            return ins
                        s0 = (128 * d * j) // SP
                        qk = [("QT", c, sx, hh) for sx in range(8)] if d == 16 else [("QT", c, s0, hh)]
                        kk = [("KT", c, sx) for sx in range(8)] if d == 16 else [("KT", c, s0), ("KT", c, max(s0 - 1, 0))]
                        T.add("pe", f, reads=qk + kk, writes=[("ST", st)])
                        T.add("act", lambda e: e.activation(out=PTr[pt][:, 0:ncol], in_=pbank[bank][:, half:half + ncol], func=AF.Exp),
                              reads=[("ST", st)], writes=[("PT", pt)])
                        T.add("dve", lambda e: e.tensor_tensor(out=PTr[pt][:, 0:ncol], in0=PTr[pt][:, 0:ncol],
                                                               in1=biasb[:, e_idx * 256: e_idx * 256 + ncol], op=ALU.mult),
                              reads=[("PT", pt), "biasb"], writes=[("PT", pt)])

                    def emit_pv(t):
                        d, r, j = t["d"], t["r"], t["j"]
                        if t["qi"] == 0:
                            state["ob"] = ob[0] % 2
                            ob[0] += 1
                        obk = state["ob"]
                        pt = t["pt"]
                        qi = t["qi"]
                        if d == 1:
                            tv, tl = V1a, VLEN
                            off = lambda jj: 64 + jj * 512 + h * 64
                            vk = lambda jj: [("V1", jj)]
                            ok = ["V1ones", "V1ones2"]
                        elif d == 4:
                            tv, tl = V4c, V4LEN
                            off = lambda jj: 64 + (jj * 4 + r) * 128 + hh * 64
                            vk = lambda jj: v4keys
                            ok = ["V4ones"]
                        else:
                            tv, tl = V16, V16LEN
                            off = lambda jj: 64 + (jj * 16 + r) * 256 + (h % 4) * 64
                            vk = lambda jj: [("V16", r, a) for a in range(16)]
                            ok = ["V16ones"]
                        lc = tv[:, off(j):off(j) + 64]
                        lp = tv[:, off(j - 1):off(j - 1) + 64] if j > 0 else None
                        np0, dp0 = (0, 64) if hh == 0 else (64, 0)

                        nprev = t["nprev"]
                        batched = nprev in (0, 4)
                        ptg = t["ptg"]

                        def f(e):
                            on = pbank[obk][np0:np0 + 64, qi * 128:(qi + 1) * 128]
                            od = pbank[obk][dp0:dp0 + 64, qi * 128:(qi + 1) * 128]
                            ins = e.matmul(on, lhsT=lc, rhs=PTr[pt][:, 0:128], start=(qi == 0), stop=(lp is None), skip_group_check=True)
                            if lp is not None:
                                ins = e.matmul(on, lhsT=lp, rhs=PTr[pt][:, 128:256], start=False, stop=True, skip_group_check=True)
                            if not batched:
                                ins = e.matmul(od, lhsT=onesb[:, 0:64], rhs=PTr[pt][:, 0:128], start=(qi == 0), stop=(lp is None), skip_group_check=True)
                                if lp is not None:
                                    ins = e.matmul(od, lhsT=onesb[:, 0:64], rhs=PTr[pt][:, 128:256], start=False, stop=True, skip_group_check=True)
                            elif qi == 3:
                                odb = pbank[obk][dp0:dp0 + 64, :]
                                pg = v3(PTg[ptg], 4)
                                ins = e.matmul(odb, lhsT=onesb[:, 0:64], rhs=pg[:, :, 0:128], start=True, stop=(nprev == 0), skip_group_check=True)
                                if nprev == 4:
                                    ins = e.matmul(odb, lhsT=onesb[:, 0:64], rhs=pg[:, :, 128:256], start=False, stop=True, skip_group_check=True)
                            return ins
                        ptk = [("PT", ptg * 4 + i) for i in range(4)] if (batched and qi == 3) else [("PT", pt)]
                        T.add("pe", f, reads=ptk + vk(j) + (vk(j - 1) if j > 0 and d == 1 else []) + ok + ["onesb"], writes=[("OB", obk)])
                        if t["last"]:
                            gi = t["gi"]
                            if d == 1:
                                a_ap = acc[asl][:, gi * 512:(gi + 1) * 512]
                                T.add("act", lambda e: e.activation(out=a_ap, in_=pbank[obk][:, :], func=AF.Copy),
                                      reads=[("OB", obk)], writes=[("acc", asl)])
                            else:
                                if d == 4:
                                    a_ap = mk_ap(acc[asl], (0, 1), 128, gi * 512, [[1, 4], [4, 128]])
                                else:
                                    a_ap = mk_ap(acc[asl], (0, 1), 128, gi * 4, [[1, 4], [16, 128]])
                                T.add("dve", lambda e: e.tensor_tensor(out=a_ap, in0=v3(pbank[obk][:, :], 4), in1=a_ap, op=ALU.add),
                                      reads=[("OB", obk), ("acc", asl)], writes=[("acc", asl)])

                    for i in range(len(tasks) + LAG):
                        if i < len(tasks):
                            emit_st(tasks[i])
                        if i >= LAG:
                            emit_pv(tasks[i - LAG])
                    np0, dp0 = (0, 64) if hh == 0 else (64, 0)
                    T.add("act", lambda e, asl=asl, np0=np0, dp0=dp0: e.activation(out=rden[np0:np0 + 64, :], in_=acc[asl][dp0:dp0 + 64, :], func=AF.Ln),
                          reads=[("acc", asl)], writes=[("rden", hh)])
                    T.add("act", lambda e, np0=np0: e.activation(out=rden[np0:np0 + 64, :], in_=rden[np0:np0 + 64, :], func=AF.Exp, scale=-1.0),
                          reads=[("rden", hh)], writes=[("rden", hh)])
                    wk = [("QT", c, sx, hh) for sx in range(4 * n, 4 * n + 4)]
                    T.add("pool", lambda e, asl=asl, np0=np0, c=c, n=n: e.tensor_tensor(
                        out=QT[np0:np0 + 64, c * S + n * 2048: c * S + (n + 1) * 2048], in0=acc[asl][np0:np0 + 64, :],
                        in1=rden[np0:np0 + 64, :], op=ALU.mult),
                        reads=[("acc", asl), ("rden", hh)], writes=[("att", c, hh, n)] + wk)
        T.barrier()

        T.add("dve", lambda e: e.memset(CG, 0.0), writes=["CG"])
        rc = [0]

        def ring_load(src_ap, nel, key_reads):
            sl = rc[0] % 5
            rc[0] += 1
            T.add("sp", lambda e: e.dma_start(out=ring[sl][:, 0:nel], in_=src_ap), reads=key_reads, writes=[("ring", sl)],
                  dma=True, semkey=("ring", sl))
            return sl

        def load_xC(s):
            for b in range(4):
                gb = 4 * s + b
                T.add("sp", lambda e, b=b, gb=gb: e.dma_start(out=xst[:, b * D:(b + 1) * D], in_=x[gb * 128:(gb + 1) * 128, :]),
                      writes=[("xs", b)], dma=True, semkey=("xs", b))

        def load_p(s):
            for b in range(4):
                gb = 4 * s + b
                T.add("pool", lambda e, b=b, gb=gb: e.dma_start(out=pst[:, b * 256:(b + 1) * 256], in_=p[gb * 128:(gb + 1) * 128, :]),
                      writes=[("pst", b)], dma=True, semkey=("pst", b))

        def rmsnorm(gcol, out_bf, avoid=()):
            for c in range(8):
                if c % 2 == 0:
                    T.add("act", lambda e, c=c: e.activation(out=sqb[:, c * 512:(c + 1) * 512], in_=hT[:, c * 512:(c + 1) * 512], func=AF.Square),
                          reads=[("hT", c)], writes=[("sqb", c)])
                else:
                    T.add("dve", lambda e, c=c: e.tensor_tensor(out=sqb[:, c * 512:(c + 1) * 512], in0=hT[:, c * 512:(c + 1) * 512],
                                                                in1=hT[:, c * 512:(c + 1) * 512], op=ALU.mult),
                          reads=[("hT", c)], writes=[("sqb", c)])
            bk = next_bank(avoid)

            def f(e):
                ins = None
                for c in range(8):
                    ins = e.matmul(pbank[bk][:, :], lhsT=onesb[:, :], rhs=sqb[:, c * 512:(c + 1) * 512], start=(c == 0), stop=(c == 7))
                return ins
            T.add("pe", f, reads=[("sqb", c) for c in range(8)] + ["onesb"], writes=[("pb", bk)])
            T.add("act", lambda e: e.activation(out=rsb, in_=pbank[bk][:, :], func=AF.Ln, bias=vec[:, V_EPS:V_EPS + 1], scale=1.0 / D),
                  reads=[("pb", bk), "vec"], writes=["rsb0", "rsb"])
            T.add("act", lambda e: e.activation(out=rsb, in_=rsb, func=AF.Exp, scale=-0.5), reads=["rsb0"], writes=["rsb"])
            for c in range(8):
                o = hnT[:, c * 512:(c + 1) * 512] if out_bf else hT[:, c * 512:(c + 1) * 512]
                T.add("dve", lambda e, c=c, o=o: e.scalar_tensor_tensor(out=o, in0=hT[:, c * 512:(c + 1) * 512], scalar=vec[:, gcol + c:gcol + c + 1],
                                                                       in1=rsb, op0=ALU.mult, op1=ALU.mult),
                      reads=[("hT", c), "rsb", "vec"], writes=[("hnT", c)] if out_bf else [("hT", c)])

        load_xC(0)
        cvc = [0]
        for s in range(NSPAN):
            t0 = s * SP
            load_p(s)
            for c in range(8):
                sl = ring_load(s_out[c], 1024, [("s_out", c)])
                bk = next_bank()

                def f(e, c=c, sl=sl, bk=bk, t0=t0):
                    o = pbank[bk]
                    for b in range(4):
                        e.matmul(o[:, b * 128:(b + 1) * 128], lhsT=xsta[:, b * D + c * 128: b * D + (c + 1) * 128], rhs=identf[:, :],
                                 start=(b == 0), stop=False, skip_group_check=True)
                    ins = None
                    for kc in range(8):
                        srct = QTa if kc < 4 else PTa
                        rhs = srct[:, (kc % 4) * S + t0:(kc % 4) * S + t0 + SP]
                        ins = e.matmul(o[:, :], lhsT=ring[sl][:, kc * 128:(kc + 1) * 128], rhs=rhs, start=False, stop=(kc == 7), skip_group_check=True)
                    return ins
                T.add("pe", f, reads=[("xs", b) for b in range(4)] + ["identf", ("ring", sl)] +
                      [("att", cc, hh, s // 4) for cc in range(4) for hh in range(2)] + [("poolT", g, s) for g in range(4)],
                      writes=[("pb", bk)])
                T.add("act", lambda e, c=c, bk=bk: e.activation(out=hT[:, c * 512:(c + 1) * 512], in_=pbank[bk][:, :], func=AF.Copy),
                      reads=[("pb", bk)], writes=[("hT", c)])
            if s + 1 < NSPAN:
                load_xC(s + 1)
            rmsnorm(V_LNFFN, True)
            hk = [("hnT", c) for c in range(8)]
            NB0 = 3
            first = []
            for j in range(NB0):
                sl = ring_load(s_up[j], 2048, [("s_up", j, 0), ("s_up", j, 1)])
                first.append((sl, next_bank(), next_bank()))
            for kc in range(8):
                def f(e, kc=kc, first=first):
                    ins = None
                    for (sl, bg, bv) in first:
                        for h2, bk in ((0, bg), (1, bv)):
                            ins = e.matmul(pbank[bk][:, :], lhsT=ring[sl][:, kc * 256 + h2 * 128: kc * 256 + (h2 + 1) * 128],
                                           rhs=hnT[:, kc * 512:(kc + 1) * 512], start=(kc == 0), stop=(kc == 7), skip_group_check=True)
                    return ins
                T.add("pe", f, reads=[("hnT", kc)] + [("ring", sl) for (sl, _, _) in first],
                      writes=[("pb", b_) for (_, bg, bv) in first for b_ in (bg, bv)])
            for j in range(NJ):
                if j < NB0:
                    sl, bg, bv = first[j]
                else:
                    sl = ring_load(s_up[j], 2048, [("s_up", j, 0), ("s_up", j, 1)])
                    bg, bv = next_bank(), next_bank()
                    for h2, bk in ((0, bg), (1, bv)):
                        def f(e, h2=h2, bk=bk, sl=sl):
                            ins = None
                            for kc in range(8):
                                ins = e.matmul(pbank[bk][:, :], lhsT=ring[sl][:, kc * 256 + h2 * 128: kc * 256 + (h2 + 1) * 128],
                                               rhs=hnT[:, kc * 512:(kc + 1) * 512], start=(kc == 0), stop=(kc == 7))
                            return ins
                        T.add("pe", f, reads=hk + [("ring", sl)], writes=[("pb", bk)])
                par = cvc[0] % 2
                cvc[0] += 1
                Gs, Vs, Yg, Yv = cv[par * 2], cv[par * 2 + 1], cv[4 + par], cv[6 + par]
                kG, kV, kYg, kYv = ("cv", par * 2), ("cv", par * 2 + 1), ("cv", 4 + par), ("cv", 6 + par)
                GV3 = arf[:, 6656 + par * 1028: 6656 + (par + 1) * 1028].rearrange("p (a b) -> p a b", a=2)
                CG3 = mk_ap(CG, (0, 1), 128, 2 * j, [[2 * NJ, 2], [1, 2]])
                T.add("pool", lambda e, GV3=GV3, CG3=CG3: e.tensor_copy(out=GV3[:, :, 0:2], in_=CG3),
                      reads=["CG", ("CGc", j)], writes=[(kG, "halo"), (kV, "halo")])
                for h2, bk, Xs, kX in ((0, bg, Gs, kG), (1, bv, Vs, kV)):
                    T.add("act", lambda e, Xs=Xs, bk=bk: e.activation(out=Xs[:, 2:514], in_=pbank[bk][:, :], func=AF.Copy),
                          reads=[("pb", bk)], writes=[kX])
                T.add("pool", lambda e, GV3=GV3, CG3=CG3: e.tensor_copy(out=CG3, in_=GV3[:, :, 512:514]),
                      reads=[kG, kV], writes=[("CGc", j)])
                for tap in range(3):
                    for h2, Xs, kX, Y, kY in ((0, Gs, kG, Yg, kYg), (1, Vs, kV, Yv, kYv)):
                        f_idx = h2 * NJ + j
                        wcol = V_CW + (2 - tap) * 44 + f_idx
                        src = Xs[:, 2 - tap: 514 - tap]
                        if tap == 0:
                            bkx = bg if h2 == 0 else bv
                            T.add("act", lambda e, bkx=bkx, Y=Y, wcol=wcol, f_idx=f_idx: e.activation(
                                out=Y[:, 0:512], in_=pbank[bkx][:, :], func=AF.Identity, bias=vec[:, V_CB + f_idx:V_CB + f_idx + 1],
                                scale=vec[:, wcol:wcol + 1]), reads=[("pb", bkx), "vec"], writes=[kY])
                        else:
                            T.add("dve", lambda e, src=src, Y=Y, wcol=wcol: e.scalar_tensor_tensor(
                                out=Y[:, 0:512], in0=src, scalar=vec[:, wcol:wcol + 1], in1=Y[:, 0:512], op0=ALU.mult, op1=ALU.add),
                                reads=[kX, (kX, "halo"), kY, "vec"], writes=[kY])
                T.add("act", lambda e, Yg=Yg: e.activation(out=Yg[:, 0:512], in_=Yg[:, 0:512], func=AF.Silu), reads=[kYg], writes=[kYg])
                T.add("pool", lambda e, Yg=Yg, Yv=Yv, j=j: e.tensor_tensor(out=actT[:, j * 512:(j + 1) * 512], in0=Yg[:, 0:512], in1=Yv[:, 0:512], op=ALU.mult),
                      reads=[kYg, kYv], writes=[("actT", j)])
            ak = [("actT", j) for j in range(NJ)]
            ND0 = 4
            firstd = []
            for c in range(ND0):
                sl = ring_load(s_down[c], NJ * 128, [("s_down", c)])
                firstd.append((sl, next_bank()))
            for kc in range(NJ):
                def f(e, kc=kc, firstd=firstd):
                    ins = None
                    for (sl, bk) in firstd:
                        ins = e.matmul(pbank[bk][:, :], lhsT=ring[sl][:, kc * 128:(kc + 1) * 128], rhs=actT[:, kc * 512:(kc + 1) * 512],
                                       start=(kc == 0), stop=(kc == NJ - 1), skip_group_check=True)
                    return ins
                T.add("pe", f, reads=[("actT", kc)] + [("ring", sl) for (sl, _) in firstd], writes=[("pb", bk) for (_, bk) in firstd])
            for c in range(8):
                if c < ND0:
                    bk = firstd[c][1]
                else:
                    sl = ring_load(s_down[c], NJ * 128, [("s_down", c)])
                    bk = next_bank()

                    def f(e, sl=sl, bk=bk):
                        ins = None
                        for kc in range(NJ):
                            ins = e.matmul(pbank[bk][:, :], lhsT=ring[sl][:, kc * 128:(kc + 1) * 128], rhs=actT[:, kc * 512:(kc + 1) * 512],
                                           start=(kc == 0), stop=(kc == NJ - 1))
                        return ins
                    T.add("pe", f, reads=ak + [("ring", sl)], writes=[("pb", bk)])
                T.add("dve", lambda e, c=c, bk=bk: e.tensor_tensor(out=hT[:, c * 512:(c + 1) * 512], in0=pbank[bk][:, :], in1=hT[:, c * 512:(c + 1) * 512], op=ALU.add),
                      reads=[("pb", bk), ("hT", c)], writes=[("hT", c)])
            for kc in range(2):
                bk = next_bank()

                def f(e, kc=kc, bk=bk):
                    ins = None
                    for b in range(4):
                        ins = e.matmul(pbank[bk][:, b * 128:(b + 1) * 128], lhsT=pst[:, b * 256 + kc * 128: b * 256 + (kc + 1) * 128], rhs=identf[:, :],
                                       start=(b == 0), stop=(b == 3), skip_group_check=True)
                    return ins
                T.add("pe", f, reads=[("pst", b) for b in range(4)] + ["identf"], writes=[("pb", bk)])
                T.add("act", lambda e, kc=kc, bk=bk: e.activation(out=pT[:, kc * 512:(kc + 1) * 512], in_=pbank[bk][:, :], func=AF.Copy),
                      reads=[("pb", bk)], writes=[("pT", kc)])

            def ple_mm(c):
                slp = ring_load(s_ple[c], 256, [("s_ple", c)])
                bp = next_bank()

                def fp(e):
                    ins = None
                    for kc in range(2):
                        ins = e.matmul(pbank[bp][:, :], lhsT=ring[slp][:, kc * 128:(kc + 1) * 128], rhs=pT[:, kc * 512:(kc + 1) * 512],
                                       start=(kc == 0), stop=(kc == 1))
                    return ins
                T.add("pe", fp, reads=[("pT", 0), ("pT", 1), ("ring", slp)], writes=[("pb", bp)])
                return bp
            NG0 = 3
            bps = [ple_mm(c) for c in range(NG0)]
            rmsnorm(V_LNPLE, True, avoid=bps)
            firstg = []
            for c in range(NG0):
                slg = ring_load(s_gate[c], 1024, [("s_gate", c)])
                firstg.append((slg, next_bank(avoid=bps)))
            for kc in range(8):
                def f(e, kc=kc, firstg=firstg):
                    ins = None
                    for (slg, bg) in firstg:
                        ins = e.matmul(pbank[bg][:, :], lhsT=ring[slg][:, kc * 128:(kc + 1) * 128], rhs=hnT[:, kc * 512:(kc + 1) * 512],
                                       start=(kc == 0), stop=(kc == 7), skip_group_check=True)
                    return ins
                T.add("pe", f, reads=[("hnT", kc)] + [("ring", slg) for (slg, _) in firstg], writes=[("pb", bg) for (_, bg) in firstg])
            for c in range(8):
                if c < NG0:
                    bg, bp = firstg[c][1], bps[c]
                else:
                    slg = ring_load(s_gate[c], 1024, [("s_gate", c)])
                    bg = next_bank()

                    def fg(e, slg=slg, bg=bg):
                        ins = None
                        for kc in range(8):
                            ins = e.matmul(pbank[bg][:, :], lhsT=ring[slg][:, kc * 128:(kc + 1) * 128], rhs=hnT[:, kc * 512:(kc + 1) * 512],
                                           start=(kc == 0), stop=(kc == 7))
                        return ins
                    T.add("pe", fg, reads=hk + [("ring", slg)], writes=[("pb", bg)])
                    bp = ple_mm(c)
                sg = sgt[c % 2]
                ksg = ("cv", 4 + c % 2)
                T.add("act", lambda e, sg=sg, bg=bg: e.activation(out=sg[:, 0:512], in_=pbank[bg][:, :], func=AF.Sigmoid),
                      reads=[("pb", bg)], writes=[ksg])
                T.add("dve", lambda e, sg=sg, bp=bp: e.tensor_tensor(out=sg[:, 0:512], in0=sg[:, 0:512], in1=pbank[bp][:, :], op=ALU.mult),
                      reads=[ksg, ("pb", bp)], writes=[ksg])
                T.add("dve", lambda e, sg=sg, c=c: e.tensor_tensor(out=hT[:, c * 512:(c + 1) * 512], in0=sg[:, 0:512], in1=hT[:, c * 512:(c + 1) * 512], op=ALU.add),
                      reads=[ksg, ("hT", c)], writes=[("hT", c)])
            rmsnorm(V_LNFIN, False)
            for b in range(4):
                gb = 4 * s + b
                osl = gb % 2

                def f(e, b=b):
                    ins = None
                    for c in range(8):
                        ins = e.transpose(out=pbig[:, c * 128:(c + 1) * 128], in_=hT[:, c * 512 + b * 128: c * 512 + (b + 1) * 128], identity=identf[:, :])
                    return ins
                T.add("pe", f, reads=[("hT", c) for c in range(8)] + ["identf"], writes=["pbig"])
                if b % 2 == 0:
                    T.add("act", lambda e, osl=osl: e.activation(out=ost[osl], in_=pbig[:, :], func=AF.Copy), reads=["pbig"], writes=[("ost", osl)])
                else:
                    T.add("dve", lambda e, osl=osl: e.tensor_copy(out=ost[osl], in_=pbig[:, :]), reads=["pbig"], writes=[("ost", osl)])
                T.add("pool", lambda e, osl=osl, gb=gb: e.dma_start(out=out[gb * 128:(gb + 1) * 128, :], in_=ost[osl]),
                      reads=[("ost", osl)], writes=[("outd", osl)], dma=True, semkey=("outd", osl))
        T.add("sp", None, reads=[("outd", 0), ("outd", 1)])

        with nc.Block() as block:
            T.emit(nc, block, sems, new_sem)
    return nc


def _consts():
    ident = np.eye(128, dtype=np.float32)
    k = np.arange(128)[:, None].astype(np.float64)
    q = np.arange(128)[None, :].astype(np.float64)
    tabs = []
    for e in range(12):
        a = 2.0 ** (e - 8)
        cur = np.where(k <= q, np.exp(-a * (q - k)), 0.0)
        prev = np.where(k >= q, np.exp(-a * (q + 128 - k)), 0.0)
        tabs.append(np.concatenate([cur, prev], axis=1))
    bias_tab = np.concatenate(tabs, axis=1).astype(np.float32)
    return ident, bias_tab


def _vecs(ln_mix, ln_ffn, ln_ple, ln_final, pool_scale, conv_w, conv_b):
    v = np.zeros((128, NV), dtype=np.float32)
    v[:, V_LNMIX:V_LNMIX + 8] = ln_mix.reshape(8, 128).T
    v[:, V_LNFFN:V_LNFFN + 8] = ln_ffn.reshape(8, 128).T
    v[:, V_LNPLE:V_LNPLE + 8] = ln_ple.reshape(8, 128).T
    v[:, V_LNFIN:V_LNFIN + 8] = ln_final.reshape(8, 128).T
    v[:, V_PSC:V_PSC + 4] = pool_scale.reshape(4, 128).T
    for kk in range(3):
        v[:, V_CW + kk * 44: V_CW + (kk + 1) * 44] = conv_w[kk].reshape(44, 128).T
    v[:, V_CB:V_CB + 44] = conv_b.reshape(44, 128).T
    v[:, V_CNT:V_CNT + 16] = (1.0 / np.arange(1, 17, dtype=np.float32))[None, :]
    v[:, V_EPS] = EPS
    return v


def kernel(x, p, ln_mix, w_in, pool_w, pool_scale, w_out, ln_ffn, w_up, conv_w, conv_b,
           w_down, ln_ple, w_ple_gate, w_ple, ln_final):
    f = lambda a: np.ascontiguousarray(np.asarray(a, dtype=np.float32))
    x, p = f(x), f(p)
    ident, bias_tab = _consts()
    vecs = _vecs(f(ln_mix)[0], f(ln_ffn)[0], f(ln_ple)[0], f(ln_final), f(pool_scale)[0], f(conv_w)[0], f(conv_b)[0])
    shared = {"w_in": f(w_in)[0], "w_out": f(w_out)[0], "w_up": f(w_up)[0], "w_down": f(w_down)[0], "w_gate": f(w_ple_gate)[0],
              "w_ple": f(w_ple)[0], "pool_w": f(pool_w)[0], "vecs": vecs, "ident": ident, "bias_tab": bias_tab}
    nc = build_program()
    in_maps = []
    for b in range(8):
        m = dict(shared)
        m["x"] = np.ascontiguousarray(x[b])
        m["p"] = np.ascontiguousarray(p[0, b])
        in_maps.append(m)
    res = run_bass_kernel_spmd(nc, in_maps, core_ids=list(range(8)))
    return np.stack([np.asarray(r["out"], dtype=np.float32) for r in res.results], axis=0)
```
